# Optimizing a Trainium2 kernel written in Bass

```python
import jax, jax.numpy as jnp
from jax import lax
import numpy as np

D_MODEL = 1024
BATCH = 8
SEQ = 8192
DEPTH = 2

D_MIX = D_MODEL
D_RNN = D_MIX // 2
RNN_HEADS = 8
RNN_HEAD_DIM = D_RNN // RNN_HEADS
D_POOL = D_MIX // 4
POOL_WINDOWS = (2, 4, 8, 16)
POOL_GROUPS = len(POOL_WINDOWS)
POOL_GROUP_DIM = D_POOL // POOL_GROUPS
D_SGU = D_MIX // 4
SGU_HEADS = 4
SGU_HEAD_DIM = D_SGU // SGU_HEADS
CHUNK = 128
CONV_WIDTH = 4
LRU_C = 8.0
D_IN = 2 * D_RNN + D_POOL + 2 * D_SGU
D_FF = 64 * ((8 * D_MODEL // 3 + 63) // 64)
EPS = 1e-6

kernel_name = "hybrid_rglru_pool_sgu_macaron"


def rmsnorm(x, g):
    x32 = x.astype(jnp.float32)
    y = x32 * lax.rsqrt(jnp.mean(x32 * x32, axis=-1, keepdims=True) + EPS)
    return (y * g.astype(jnp.float32)).astype(x.dtype)


def swiglu(h, w_in, w_out):
    g, u = jnp.split(h @ w_in, 2, axis=-1)
    return (jax.nn.silu(g) * u) @ w_out


def causal_dwconv(x, w, b):
    S = x.shape[1]
    xp = jnp.pad(x, ((0, 0), (CONV_WIDTH - 1, 0), (0, 0)))
    y = b
    for k in range(CONV_WIDTH):
        y = y + xp[:, k:k + S] * w[k]
    return y


def rglru_branch(gate, xa, conv_w, conv_b, w_a, b_a, w_x, b_x, lam):
    B, S, _ = xa.shape
    xc = causal_dwconv(xa, conv_w, conv_b)
    xh = xc.reshape(B, S, RNN_HEADS, RNN_HEAD_DIM)
    r = jax.nn.sigmoid(jnp.einsum('bshi,hij->bshj', xh, w_a) + b_a)
    i = jax.nn.sigmoid(jnp.einsum('bshi,hij->bshj', xh, w_x) + b_x)
    r32 = r.astype(jnp.float32).reshape(B, S, D_RNN)
    i32 = i.astype(jnp.float32).reshape(B, S, D_RNN)
    x32 = xc.astype(jnp.float32)
    log_a = -LRU_C * r32 * jax.nn.softplus(-lam.astype(jnp.float32))
    a = jnp.exp(log_a)
    mult = jnp.sqrt(-jnp.expm1(2.0 * log_a))
    bvals = mult * (i32 * x32)

    def combine(left, right):
        a_l, b_l = left
        a_r, b_r = right
        return a_l * a_r, a_r * b_l + b_r

    _, h = lax.associative_scan(combine, (a, bvals), axis=1)
    return jax.nn.gelu(gate) * h.astype(gate.dtype)


def causal_window_mean(x32, w):
    S = x32.shape[1]
    cs = jnp.cumsum(x32, axis=1)
    prev = jnp.pad(cs, ((0, 0), (w, 0), (0, 0)))[:, :S]
    count = jnp.minimum(jnp.arange(S) + 1, w).astype(jnp.float32)
    return (cs - prev) / count[None, :, None]


def pool_branch(xp, pool_w, pool_scale):
    x32 = xp.astype(jnp.float32)
    outs = []
    for g, w in enumerate(POOL_WINDOWS):
        xg = x32[..., g * POOL_GROUP_DIM:(g + 1) * POOL_GROUP_DIM]
        d = (causal_window_mean(xg, w) - xg).astype(xp.dtype)
        outs.append(d @ pool_w[g])
    return jnp.concatenate(outs, axis=-1) * pool_scale


def sgu_branch(u, v, sgu_norm, sgu_w, sgu_b):
    B, S, _ = u.shape
    u = jax.nn.gelu(u)
    v = rmsnorm(jax.nn.gelu(v), sgu_norm)
    vh = v.reshape(B, S // CHUNK, CHUNK, SGU_HEADS, SGU_HEAD_DIM)
    mask = jnp.tril(jnp.ones((CHUNK, CHUNK), dtype=bool))
    ws = jnp.where(mask[None], sgu_w, jnp.zeros_like(sgu_w))
    z = jnp.einsum('hts,bnshd->bnthd', ws, vh) + jnp.transpose(sgu_b)[None, None, :, :, None]
    return u * z.reshape(B, S, D_SGU)


def setup_inputs(seed: int = 0) -> dict:
    key = jax.random.key(seed)
    ks = jax.random.split(key, 24)
    f32 = jnp.float32

    def nrm(k, shape, scale):
        return jax.random.normal(k, shape, f32) * scale

    def gain(k, shape):
        return 1.0 + 0.05 * jax.random.normal(k, shape, f32)

    u_a = jax.random.uniform(ks[9], (DEPTH, D_RNN), f32, 0.9, 0.999)
    s = u_a ** (1.0 / LRU_C)
    lru_lambda = jnp.log(s) - jnp.log1p(-s)

    return {
        "x": jax.random.normal(ks[0], (BATCH, SEQ, D_MODEL), f32),
        "ffn1_norm": gain(ks[1], (DEPTH, D_MODEL)),
        "ffn1_w_in": nrm(ks[2], (DEPTH, D_MODEL, 2 * D_FF), D_MODEL ** -0.5),
        "ffn1_w_out": nrm(ks[3], (DEPTH, D_FF, D_MODEL), D_FF ** -0.5),
        "mix_norm": gain(ks[4], (DEPTH, D_MODEL)),
        "w_in": nrm(ks[5], (DEPTH, D_MODEL, D_IN), D_MODEL ** -0.5),
        "conv_w": nrm(ks[6], (DEPTH, CONV_WIDTH, D_RNN), CONV_WIDTH ** -0.5),
        "conv_b": nrm(ks[7], (DEPTH, D_RNN), 0.02),
        "rg_w_a": nrm(ks[8], (DEPTH, RNN_HEADS, RNN_HEAD_DIM, RNN_HEAD_DIM), RNN_HEAD_DIM ** -0.5),
        "rg_b_a": nrm(ks[10], (DEPTH, RNN_HEADS, RNN_HEAD_DIM), 0.02),
        "rg_w_x": nrm(ks[11], (DEPTH, RNN_HEADS, RNN_HEAD_DIM, RNN_HEAD_DIM), RNN_HEAD_DIM ** -0.5),
        "rg_b_x": nrm(ks[12], (DEPTH, RNN_HEADS, RNN_HEAD_DIM), 0.02),
        "lru_lambda": lru_lambda,
        "pool_w": nrm(ks[13], (DEPTH, POOL_GROUPS, POOL_GROUP_DIM, POOL_GROUP_DIM), POOL_GROUP_DIM ** -0.5),
        "pool_scale": gain(ks[14], (DEPTH, D_POOL)),
        "sgu_norm": gain(ks[15], (DEPTH, D_SGU)),
        "sgu_w": nrm(ks[16], (DEPTH, SGU_HEADS, CHUNK, CHUNK), CHUNK ** -0.5),
        "sgu_b": gain(ks[17], (DEPTH, SGU_HEADS, CHUNK)),
        "w_out": nrm(ks[18], (DEPTH, D_MIX, D_MODEL), D_MIX ** -0.5),
        "ffn2_norm": gain(ks[19], (DEPTH, D_MODEL)),
        "ffn2_w_in": nrm(ks[20], (DEPTH, D_MODEL, 2 * D_FF), D_MODEL ** -0.5),
        "ffn2_w_out": nrm(ks[21], (DEPTH, D_FF, D_MODEL), D_FF ** -0.5),
        "final_norm": gain(ks[22], (D_MODEL,)),
    }


def reference(x, ffn1_norm, ffn1_w_in, ffn1_w_out, mix_norm, w_in, conv_w, conv_b,
              rg_w_a, rg_b_a, rg_w_x, rg_b_x, lru_lambda, pool_w, pool_scale,
              sgu_norm, sgu_w, sgu_b, w_out, ffn2_norm, ffn2_w_in, ffn2_w_out, final_norm):
    s1 = D_RNN
    s2 = 2 * D_RNN
    s3 = s2 + D_POOL
    s4 = s3 + D_SGU
    for l in range(DEPTH):
        x = x + 0.5 * swiglu(rmsnorm(x, ffn1_norm[l]), ffn1_w_in[l], ffn1_w_out[l])
        h = rmsnorm(x, mix_norm[l])
        p = h @ w_in[l]
        gate_a, xa, xp, u, v = jnp.split(p, [s1, s2, s3, s4], axis=-1)
        ya = rglru_branch(gate_a, xa, conv_w[l], conv_b[l], rg_w_a[l], rg_b_a[l],
                          rg_w_x[l], rg_b_x[l], lru_lambda[l])
        yb = pool_branch(xp, pool_w[l], pool_scale[l])
        yc = sgu_branch(u, v, sgu_norm[l], sgu_w[l], sgu_b[l])
        x = x + jnp.concatenate([ya, yb, yc], axis=-1) @ w_out[l]
        x = x + 0.5 * swiglu(rmsnorm(x, ffn2_norm[l]), ffn2_w_in[l], ffn2_w_out[l])
    return rmsnorm(x, final_norm)
```

```python
import numpy as np
from contextlib import ExitStack
import concourse.bass as bass
import concourse.mybir as mybir
from concourse.bass_utils import run_bass_kernel_spmd

F32 = mybir.dt.float32
BF16 = mybir.dt.bfloat16
ALU = mybir.AluOpType
AF = mybir.ActivationFunctionType

D = 1024
KC = 8
SEQ = 8192
T = 1024
S = 512
NS = 2
L = 2
DFF = 2752
FC = 22
DIN = 1792
EPS = 1e-6
PL = 60
NB = 8
PIECE = 2048
WB_N = 1408

ENGS = ("pe", "act", "dve", "pool", "sp")


class Buf:
    __slots__ = ("name", "last_w", "readers")

    def __init__(self, name):
        self.name = name
        self.last_w = None
        self.readers = []


class Tl:
    __slots__ = ("ap", "bufs")

    def __init__(self, ap, bufs):
        self.ap = ap
        self.bufs = bufs


class DSem:
    def __init__(self, handle):
        self.h = handle
        self.count = 0


class Prog:
    def __init__(self, nc):
        self.nc = nc
        self.ins = {e: [] for e in ENGS}
        self.waited = {e: {} for e in ENGS}
        self.n_wait = 0

    def _add(self, eng, fn, reads, writes, dma_sem=None):
        rec = {"fn": fn, "waits": [], "mark": False, "dma_sem": dma_sem}
        idx = len(self.ins[eng])
        if dma_sem is not None:
            dma_sem.count += 16
            tok = ("dma", dma_sem, dma_sem.count)
        else:
            tok = ("eng", eng, idx)
        rb = []
        for t in reads:
            rb.extend(t.bufs)
        wbs = []
        for t in writes:
            wbs.extend(t.bufs)
        best = {}

        def consider(d):
            if d[0] == "eng":
                if d[1] == eng and dma_sem is None:
                    if eng == "pe":
                        return
                key = ("eng", d[1])
            else:
                key = ("dma", id(d[1]))
            v = d[2]
            if key not in best or best[key][1] < v:
                best[key] = (d, v)

        for b in rb:
            if b.last_w is not None:
                consider(b.last_w)
        for b in wbs:
            if b.last_w is not None:
                consider(b.last_w)
            for r in b.readers:
                consider(r)
        w = self.waited[eng]
        for key, (d, v) in best.items():
            if w.get(key, -1) >= v:
                continue
            w[key] = v
            rec["waits"].append(d)
            if d[0] == "eng":
                self.ins[d[1]][d[2]]["mark"] = True
        self.ins[eng].append(rec)
        for b in rb:
            b.readers.append(tok)
        for b in wbs:
            b.last_w = tok
            b.readers = []
        return tok

    def op(self, eng, fn, reads=(), writes=()):
        return self._add(eng, fn, list(reads), list(writes))

    def dma(self, eng, out, in_, dsem, reads=(), writes=()):
        def fn(e):
            return e.dma_start(out=out, in_=in_)
        return self._add(eng, fn, list(reads), list(writes), dma_sem=dsem)

    def emit(self, final_waits=()):
        nc = self.nc
        sems = {e: nc.alloc_semaphore(name=f"sem_{e}") for e in ENGS}
        for e in ENGS:
            c = 0
            for rec in self.ins[e]:
                if rec["mark"]:
                    c += 1
                    rec["val"] = c
        prog = self

        def run(eng_name, eng):
            for rec in prog.ins[eng_name]:
                for d in rec["waits"]:
                    if d[0] == "eng":
                        eng.wait_ge(sems[d[1]], prog.ins[d[1]][d[2]]["val"])
                    else:
                        eng.wait_ge(d[1].h, d[2])
                    prog.n_wait += 1
                inst = rec["fn"](eng)
                if rec["dma_sem"] is not None:
                    inst.then_inc(rec["dma_sem"].h, 16)
                elif rec["mark"]:
                    inst.then_inc(sems[eng_name], 1)
            if eng_name == "sp":
                for (ds, v) in final_waits:
                    eng.wait_ge(ds.h, v)

        with nc.Block() as block:
            @block.sync
            def _(e):
                run("sp", e)

            @block.tensor
            def _(e):
                run("pe", e)

            @block.scalar
            def _(e):
                run("act", e)

            @block.vector
            def _(e):
                run("dve", e)

            @block.gpsimd
            def _(e):
                run("pool", e)


def build(NT=SEQ // T, NL=L, nstage=None, branches="abc"):
    nc = bass.Bass("TRN2", target_bir_lowering=False)
    P = Prog(nc)
    es = ExitStack()
    ntok = NT * T

    xT_d = nc.dram_tensor("xT", [D, ntok], F32, kind="ExternalInput").ap()
    wA_d = nc.dram_tensor("wA", [L, 2, FC, 128, PIECE], F32, kind="ExternalInput").ap()
    wB_d = nc.dram_tensor("wB", [L, 2, 16, 128, WB_N], F32, kind="ExternalInput").ap()
    wC_d = nc.dram_tensor("wC", [L, 7, 128, PIECE], F32, kind="ExternalInput").ap()
    wD_d = nc.dram_tensor("wD", [L, 4, 128, PIECE], F32, kind="ExternalInput").ap()
    wS_d = nc.dram_tensor("wS", [128, L * 14 * 128], F32, kind="ExternalInput").ap()
    pp_d = nc.dram_tensor("pp", [128, 2 * PL + 8], F32, kind="ExternalInput").ap()
    bias_d = nc.dram_tensor("biasT", [128, L * 2 * 128], F32, kind="ExternalInput").ap()
    cst_d = nc.dram_tensor("cst", [128, 128 + 2 + 32], F32, kind="ExternalInput").ap()
    yT_d = nc.dram_tensor("yT", [D, ntok], F32, kind="ExternalOutput").ap()
    sA_d = nc.dram_tensor("sA", [L, 2, FC, 128, PIECE], BF16, kind="Internal").ap()
    sB_d = nc.dram_tensor("sB", [L, 2, 16, 128, WB_N], BF16, kind="Internal").ap()
    sC_d = nc.dram_tensor("sC", [L, 7, 128, PIECE], BF16, kind="Internal").ap()
    sD_d = nc.dram_tensor("sD", [L, 4, 128, PIECE], BF16, kind="Internal").ap()

    def sb(name, shape, dt):
        return es.enter_context(nc.sbuf_tensor(name, shape, dt))

    def newsem(name):
        return DSem(nc.alloc_semaphore(name=name))

    def mk(ap, name):
        return Tl(ap, [Buf(name)])

    x_t = sb("x_res", [128, KC, T], F32)
    X = [[mk(x_t[:, k, s * S:(s + 1) * S], f"x{k}_{s}") for s in range(NS)] for k in range(KC)]
    h_t = sb("h_bf", [128, KC, T], BF16)
    H = [[mk(h_t[:, k, s * S:(s + 1) * S], f"h{k}_{s}") for s in range(NS)] for k in range(KC)]
    ring_t = sb("ring", [128, NB, PIECE], BF16)
    RING = [mk(ring_t[:, i, :], f"ring{i}") for i in range(NB)]
    stg_t = sb("stg", [128, 2, PIECE], F32)
    STG = [mk(stg_t[:, i, :], f"stg{i}") for i in range(2)]
    ws_t = sb("ws_bf", [128, L * 14 * 128], BF16)
    WS = mk(ws_t[:], "ws")
    pp_t = sb("pp_sb", [128, 2 * PL + 8], F32)
    PP = mk(pp_t[:], "pp")
    der_t = sb("der", [128, L * 16], F32)
    DER = mk(der_t[:], "der")
    bias_t = sb("bias_sb", [128, L * 2 * 128], F32)
    BIAS = mk(bias_t[:], "bias")
    cst_t = sb("cst_sb", [128, 128 + 2 + 32], F32)
    CST = mk(cst_t[:], "cst")
    ident_t = sb("ident_bf", [128, 128], BF16)
    IDENT = mk(ident_t[:], "ident")
    ones_t = sb("ones_bf", [128, 128], BF16)
    ONES = mk(ones_t[:], "ones")
    xsq_t = sb("xsq", [128, KC, S], BF16)
    XSQ = [mk(xsq_t[:, k, :], f"xsq{k}") for k in range(KC)]
    rs_t = sb("rs", [128, 2, S], F32)
    RS = [mk(rs_t[:, i, :], f"rs{i}") for i in range(2)]
    hista_t = sb("hista", [128, L, 4, 4], F32)
    HISTA = [mk(hista_t[:, l], f"hista{l}") for l in range(L)]
    histb_t = sb("histb", [128, L, 2, 16], F32)
    HISTB = [mk(histb_t[:, l], f"histb{l}") for l in range(L)]
    hst_t = sb("hstate", [128, L, 4], F32)
    HST = [mk(hst_t[:, l], f"hst{l}") for l in range(L)]

    ARENA_KB = 84
    ar_t = sb("arena", [128, ARENA_KB * 256], F32)
    ar_bufs = [Buf(f"ar{i}") for i in range(ARENA_KB)]

    def carve(off_b, shape_free, dt):
        esz = 4 if dt == F32 else 2
        n = int(np.prod(shape_free))
        nbytes = n * esz
        assert off_b % 4 == 0
        c0 = off_b // 4
        ncol32 = (nbytes + 3) // 4
        ap = ar_t[:, c0:c0 + ncol32]
        if dt != F32:
            ap = ap.bitcast(dt)[:, 0:n]
        if len(shape_free) == 2:
            ap = ap.rearrange("p (a b) -> p a b", a=shape_free[0])
        b0 = off_b // 1024
        b1 = (off_b + nbytes - 1) // 1024
        assert b1 < ARENA_KB, (off_b, nbytes)
        return Tl(ap, ar_bufs[b0:b1 + 1])

    class Alloc:
        def __init__(self, base=0):
            self.off = base

        def get(self, shape_free, dt):
            t = carve(self.off, shape_free, dt)
            esz = 4 if dt == F32 else 2
            nb = int(np.prod(shape_free)) * esz
            self.off += ((nb + 1023) // 1024) * 1024
            return t

    a1 = Alloc(0)
    SG = [a1.get([S], F32) for _ in range(2)]
    AT = [[a1.get([S], BF16) for s in range(NS)] for f in range(FC)]
    assert a1.off <= 48 * 1024
    a2 = Alloc(48 * 1024)
    YS = [[a2.get([S], F32) for s in range(NS)] for k in range(KC)]
    assert a2.off <= ARENA_KB * 1024
    a3 = Alloc(0)
    XA = [a3.get([4 + S], F32) for c in range(4)]
    XC = [a3.get([S], F32) for c in range(4)]
    XCB = [a3.get([S], BF16) for c in range(4)]
    LR = [[a3.get([S], F32) for j in range(3)] for c in range(4)]
    XB = [a3.get([16 + S], F32) for c in range(2)]
    _pt = [a3.get([16 + S], F32) for j in range(2)]
    PT = [_pt, _pt]
    DB = [a3.get([S], BF16) for c in range(2)]
    GU = [a3.get([S], F32) for c in range(2)]
    GV = [a3.get([S], F32) for c in range(2)]
    VSQ = [a3.get([S], BF16) for c in range(2)]
    VN = [a3.get([S], BF16) for c in range(2)]
    VTOK = a3.get([4, 256], BF16)
    ZT = GV
    MIX = [a3.get([S], BF16) for c in range(8)]
    assert a3.off <= ARENA_KB * 1024, a3.off

    banks = []
    for i in range(8):
        pt_ = es.enter_context(nc.psum_tensor(f"bank{i}", [128, S], F32))
        banks.append(mk(pt_[:], f"bank{i}"))
    bank_ctr = [0]

    ring_ld = [newsem(f"rl{i}") for i in range(NB)]
    ring_st = [newsem(f"rs{i}") for i in range(NB)]
    stg_ld = [newsem(f"sl{i}") for i in range(2)]
    x_ld = [[newsem(f"xl{k}_{s}") for s in range(NS)] for k in range(KC)]
    y_st = [[newsem(f"ys{k}_{s}") for s in range(NS)] for k in range(KC)]
    misc = [newsem(f"ms{i}") for i in range(5)]

    def V(eng, out, in0, in1, op, reads=None, writes=None):
        def g(t):
            return (t, t.ap) if isinstance(t, Tl) else t
        (to, ao), (t0, a0), (t1, a1_) = g(out), g(in0), g(in1)
        P.op(eng, lambda e: e.tensor_tensor(out=ao, in0=a0, in1=a1_, op=op),
             reads=[t0, t1], writes=[to])

    def TS(eng, out, in0, s1, s2, op0, op1, extra_reads=()):
        def g(t):
            return (t, t.ap) if isinstance(t, Tl) else t
        (to, ao), (t0, a0) = g(out), g(in0)
        if op1 is None:
            P.op(eng, lambda e: e.tensor_scalar(out=ao, in0=a0, scalar1=s1, scalar2=None, op0=op0),
                 reads=[t0] + list(extra_reads), writes=[to])
        else:
            P.op(eng, lambda e: e.tensor_scalar(out=ao, in0=a0, scalar1=s1, scalar2=s2, op0=op0, op1=op1),
                 reads=[t0] + list(extra_reads), writes=[to])

    def STT(out, in0, scalar, in1, op0, op1, extra_reads=()):
        def g(t):
            return (t, t.ap) if isinstance(t, Tl) else t
        (to, ao), (t0, a0), (t1, a1_) = g(out), g(in0), g(in1)
        P.op("dve", lambda e: e.scalar_tensor_tensor(out=ao, in0=a0, scalar=scalar, in1=a1_, op0=op0, op1=op1),
             reads=[t0, t1] + list(extra_reads), writes=[to])

    def ACT(out, in_, func, bias=None, scale=None, extra_reads=()):
        def g(t):
            return (t, t.ap) if isinstance(t, Tl) else t
        (to, ao), (t0, a0) = g(out), g(in_)
        kw = {}
        if bias is not None:
            kw["bias"] = bias
        if scale is not None:
            kw["scale"] = scale
        P.op("act", lambda e: e.activation(out=ao, in_=a0, func=func, **kw),
             reads=[t0] + list(extra_reads), writes=[to])

    def CP(eng, out, in_):
        def g(t):
            return (t, t.ap) if isinstance(t, Tl) else t
        (to, ao), (t0, a0) = g(out), g(in_)
        P.op(eng, lambda e: e.tensor_copy(out=ao, in_=a0), reads=[t0], writes=[to])

    def MM(out, lhsT, rhs, start, stop):
        def g(t):
            return (t, t.ap) if isinstance(t, Tl) else t
        (to, ao), (tl, al), (tr, ar) = g(out), g(lhsT), g(rhs)
        P.op("pe", lambda e: e.matmul(ao, lhsT=al, rhs=ar, start=start, stop=stop),
             reads=[tl, tr], writes=[to])

    def ppc(col):
        return pp_t[:, col:col + 1]

    P.dma("sp", pp_t[:], pp_d[:, :], misc[0], writes=[PP])
    P.dma("sp", cst_t[:], cst_d[:, :], misc[1], writes=[CST])
    P.dma("sp", bias_t[:], bias_d[:, :], misc[2], writes=[BIAS])
    nws = L * 14 * 128
    half = nws // 2
    for hh in range(2):
        P.dma("sp", stg_t[:, hh, 0:half], wS_d[:, hh * half:(hh + 1) * half], stg_ld[hh], writes=[STG[hh]])
    for l in range(L):
        for hd in range(4):
            o = (10 + hd) * 128
            P.op("dve", lambda e, l=l, o=o: e.tensor_tensor(out=stg_t[:, l, o:o + 128], in0=stg_t[:, l, o:o + 128],
                                                            in1=cst_t[:, 0:128], op=ALU.mult),
                 reads=[STG[l], CST], writes=[STG[l]])
        P.op("dve", lambda e, l=l: e.tensor_copy(out=ws_t[:, l * half:(l + 1) * half], in_=stg_t[:, l, 0:half]),
             reads=[STG[l]], writes=[WS])
    P.op("dve", lambda e: e.memset(ones_t[:], 1.0), writes=[ONES])
    for l in range(L):
        lam = pp_t[:, l * PL + 52:l * PL + 56]
        d0 = l * 16
        P.op("act", lambda e, d0=d0, lam=lam: e.activation(out=der_t[:, d0:d0 + 4], in_=lam, func=AF.Sigmoid),
             reads=[PP], writes=[DER])
        P.op("act", lambda e, d0=d0: e.activation(out=der_t[:, d0 + 4:d0 + 8], in_=der_t[:, d0:d0 + 4], func=AF.Ln),
             reads=[DER], writes=[DER])
        P.op("dve", lambda e, d0=d0: e.tensor_scalar(out=der_t[:, d0:d0 + 4], in0=der_t[:, d0 + 4:d0 + 8],
                                                     scalar1=4.0, scalar2=None, op0=ALU.mult),
             reads=[DER], writes=[DER])
        P.op("dve", lambda e, d0=d0: e.tensor_scalar(out=der_t[:, d0 + 4:d0 + 8], in0=der_t[:, d0 + 4:d0 + 8],
                                                     scalar1=8.0, scalar2=None, op0=ALU.mult),
             reads=[DER], writes=[DER])
        P.op("dve", lambda e, d0=d0, l=l: e.tensor_scalar(out=der_t[:, d0 + 8:d0 + 16], in0=pp_t[:, l * PL + 44:l * PL + 52],
                                                          scalar1=0.5, scalar2=None, op0=ALU.mult),
             reads=[PP], writes=[DER])
        P.op("pool", lambda e, l=l: e.memset(hista_t[:, l], 0.0), writes=[HISTA[l]])
        P.op("pool", lambda e, l=l: e.memset(histb_t[:, l], 0.0), writes=[HISTB[l]])
        P.op("pool", lambda e, l=l: e.memset(hst_t[:, l], 0.0), writes=[HST[l]])

    converted = set()
    state = {"issued": 0, "stg": 0}
    plan = []

    def piece_aps(pc):
        kind, l, fi, i = pc
        if kind == "A":
            return wA_d[l, fi, i], sA_d[l, fi, i], PIECE
        if kind == "B":
            return wB_d[l, fi, i], sB_d[l, fi, i], WB_N
        if kind == "C":
            return wC_d[l, i], sC_d[l, i], PIECE
        return wD_d[l, i], sD_d[l, i], PIECE

    scratch_bufs = {}

    pending_st = []
    pending_cast = []

    def flush_stores(keep=0):
        while len(pending_st) > keep:
            (scr, slot, n, sbuf) = pending_st.pop(0)
            P.dma("sp", scr, ring_t[:, slot, 0:n], ring_st[slot], reads=[RING[slot]], writes=[sbuf])

    def flush_casts(keep=0):
        while len(pending_cast) > keep:
            (q, slot, n, scr, sbuf) = pending_cast.pop(0)
            P.op("act", lambda e, q=q, slot=slot, n=n: e.activation(out=ring_t[:, slot, 0:n], in_=stg_t[:, q, 0:n], func=AF.Copy),
                 reads=[STG[q]], writes=[RING[slot]])
            pending_st.append((scr, slot, n, sbuf))

    def issue_one():
        j = state["issued"]
        pc = plan[j]
        slot = j % NB
        src, scr, n = piece_aps(pc)
        if pc not in scratch_bufs:
            scratch_bufs[pc] = Tl(None, [Buf(f"scr{pc}")])
        sbuf = scratch_bufs[pc]
        flush_stores(0)
        flush_casts(0)
        if pc not in converted:
            converted.add(pc)
            q = state["stg"] % 2
            state["stg"] += 1
            P.dma("sp", stg_t[:, q, 0:n], src, stg_ld[q], writes=[STG[q]])
            pending_cast.append((q, slot, n, scr, sbuf))
        else:
            P.dma("sp", ring_t[:, slot, 0:n], scr, ring_ld[slot], reads=[sbuf], writes=[RING[slot]])
        state["issued"] += 1

    use_ctr = [0]

    def next_piece(expect):
        j = use_ctr[0]
        assert plan[j] == expect, (plan[j], expect)
        while state["issued"] < min(len(plan), j + NB - 1):
            issue_one()
        if state["issued"] <= j + 1:
            flush_casts(0)
        use_ctr[0] += 1
        return j % NB

    stages = []
    for l in range(NL):
        stages += [(l, "f1"), (l, "mx"), (l, "f2")]
    if nstage is not None:
        stages = stages[:nstage]
    for t in range(NT):
        for (l, kind) in stages:
            if kind == "mx":
                for i in (2, 3, 4, 0, 1, 5, 6):
                    plan.append(("C", l, 0, i))
                for i in (2, 3, 4):
                    plan.append(("C", l, 0, i))
                for i in range(4):
                    plan.append(("D", l, 0, i))
                for i in (0, 1, 5, 6):
                    plan.append(("C", l, 0, i))
                for i in range(4):
                    plan.append(("D", l, 0, i))
            else:
                fi = 0 if kind == "f1" else 1
                for f in range(FC):
                    plan.append(("A", l, fi, f))
                for i in range(16):
                    plan.append(("B", l, fi, i))

    held = set()

    def nbank():
        while True:
            i = bank_ctr[0] % 8
            bank_ctr[0] += 1
            if i not in held:
                return banks[i]

    def hold_bank():
        b = nbank()
        held.add(banks.index(b))
        return b

    def release_bank(b):
        held.discard(banks.index(b))

    eps_t = sb("eps_sb", [128, 4], F32)
    CONSTS = mk(eps_t[:], "consts")
    P.op("pool", lambda e: e.memset(eps_t[:, 0:1], EPS), writes=[CONSTS])
    P.op("pool", lambda e: e.memset(eps_t[:, 1:2], 1.0), writes=[CONSTS])
    P.op("pool", lambda e: e.memset(eps_t[:, 2:3], -0.6931471805599453), writes=[CONSTS])
    EPS_AP = eps_t[:, 0:1]
    ONE_AP = eps_t[:, 1:2]
    LNH_AP = eps_t[:, 2:3]
    fx_t = sb("fx_sb", [128, 16], F32)
    FX = mk(fx_t[:], "fx")

    rs2_t = sb("rs2", [128, 2, S], F32)
    RS2 = [mk(rs2_t[:, i, :], f"rs2_{i}") for i in range(2)]

    def norm_finish(s, st, gcol, dest, dim, src=None, r=None, part="both"):
        src = X if src is None else src
        r = RS[s] if r is None else r
        if part in ("both", "act"):
            ACT(r, st, AF.Ln, bias=EPS_AP, scale=1.0 / dim, extra_reads=[CONSTS])
            ACT(r, r, AF.Exp, scale=-0.5)
        if part in ("both", "dve"):
            for k in range(KC):
                STT(dest[k][s], src[k][s], ppc(gcol + k), r, ALU.mult, ALU.mult, extra_reads=[PP])

    sq_ctr = [0]

    class NormAcc:
        def __init__(self, s, src=None):
            self.s = s
            self.src = X if src is None else src
            self.st = hold_bank()
            self.pending = []
            self.n = 0

        def feed(self, dch, eng="pool"):
            q = XSQ[sq_ctr[0] % KC]
            sq_ctr[0] += 1
            V(eng, q, self.src[dch][self.s], self.src[dch][self.s], ALU.mult)
            self.pending.append(q)

        def pump(self, keep=0):
            while len(self.pending) > keep:
                q = self.pending.pop(0)
                MM(self.st, ONES, q, self.n == 0, self.n == KC - 1)
                self.n += 1

        def finish(self, gcol, dest):
            self.pump(0)
            self.finish_rest(gcol, dest)

        def finish_rest(self, gcol, dest, r=None, part="both"):
            assert self.n == KC and not self.pending
            norm_finish(self.s, self.st, gcol, dest, D, src=self.src, r=r, part=part)
            if part in ("both", "act"):
                release_bank(self.st)

    def norm_standalone(s, gcol, dest):
        st = nbank()
        for k in range(KC):
            eng = "pool" if (s == 1 or k % 8 in (3, 6, 7)) else "dve"
            V(eng, XSQ[k], X[k][s], X[k][s], ALU.mult)
        for k in range(KC):
            MM(st, ONES, XSQ[k], k == 0, k == KC - 1)
        norm_finish(s, st, gcol, dest, D)

    def ffn(l, fi, next_gcol, next_dest, deferred=None, last=False):
        if deferred is not None:
            deferred("act")
        for f in range(FC):
            slot = next_piece(("A", l, fi, f))
            w = ring_t[:, slot, :].rearrange("p (g k c) -> p g k c", g=2, k=KC)
            for s in range(NS):
                bg = nbank()
                bu = nbank()
                for k in range(KC):
                    MM(bg, (RING[slot], w[:, 0, k, :]), H[k][s], k == 0, k == KC - 1)
                for k in range(KC):
                    MM(bu, (RING[slot], w[:, 1, k, :]), H[k][s], k == 0, k == KC - 1)
                if deferred is not None:
                    deferred("dve")
                    deferred = None
                sg = SG[(f * NS + s) % 2]
                ACT(sg, bg, AF.Silu)
                V("dve", AT[f][s], sg, bu, ALU.mult)
        xdst = YS if last else X
        accs = [NormAcc(s, src=xdst) for s in range(NS)]
        for dch in range(KC):
            slots = []
            for hf in range(2):
                slots.append(next_piece(("B", l, fi, dch * 2 + hf)))
            for s in range(NS):
                by = nbank()
                for hf in range(2):
                    slot = slots[hf]
                    w = ring_t[:, slot, 0:WB_N].rearrange("p (j c) -> p j c", j=11)
                    for j in range(11):
                        f = hf * 11 + j
                        MM(by, (RING[slot], w[:, j, :]), AT[f][s], f == 0, f == FC - 1)
                accs[0].pump()
                accs[1].pump()
                STT(xdst[dch][s], by, 0.5, X[dch][s], ALU.mult, ALU.add)
                accs[s].feed(dch)
        if last:
            for s in range(NS):
                accs[s].pump(0)
                accs[s].finish_rest(next_gcol, next_dest, r=RS2[s], part="act")
            return lambda part="both": ([accs[s].finish_rest(next_gcol, next_dest, r=RS2[s], part="dve") for s in range(NS)]
                                        if part != "act" else None)
        accs[0].finish(next_gcol, next_dest)
        accs[1].pump(0)
        return lambda part="both": accs[1].finish_rest(next_gcol, next_dest, part=part)

    C_ORDER = (2, 3, 4, 5, 6, 0, 1)

    HOIST = True

    def mixer(l, t, next_gcol, next_dest, deferred=None):
        base = l * PL
        dl = l * 16
        wsl = ws_t[:, l * half:(l + 1) * half].rearrange("p (m c) -> p m c", m=14)
        W = 16 + S

        def win_piece(i, s):
            slot = next_piece(("C", l, 0, i))
            w = ring_t[:, slot, :].rearrange("p (o k c) -> p o k c", o=2, k=KC)
            outs = []
            for o in range(2):
                bk = nbank()
                for k in range(KC):
                    MM(bk, (RING[slot], w[:, o, k, :]), H[k][s], k == 0, k == KC - 1)
                outs.append(bk)
            return outs

        def restore_hist():
            P.op("pool", lambda e: e.tensor_copy(out=ar_hist_a(), in_=hista_t[:, l, :, 1:4]),
                 reads=[HISTA[l]], writes=XA)
            P.op("pool", lambda e: e.tensor_copy(out=ar_hist_b(), in_=histb_t[:, l, :, :]),
                 reads=[HISTB[l]], writes=XB)

        def evac_front(i, bks, eng):
            for o in range(2):
                if i in (2, 3):
                    c = 2 * i + o - 4
                    dst = (XA[c], XA[c].ap[:, 4:4 + S])
                else:
                    dst = (XB[o], XB[o].ap[:, 16:W])
                if eng == "act":
                    ACT(dst, bks[o], AF.Identity)
                else:
                    CP("dve", dst, bks[o])

        def conv(c):
            xa = XA[c]
            cw = lambda tap: ppc(base + 24 + tap * 4 + c)
            TS("dve", XC[c], (xa, xa.ap[:, 4:4 + S]), cw(3), ppc(base + 40 + c), ALU.mult, ALU.add, extra_reads=[PP])
            for tap in (2, 1, 0):
                sh = 3 - tap
                STT(XC[c], (xa, xa.ap[:, 4 - sh:4 - sh + S]), cw(tap), XC[c], ALU.mult, ALU.add, extra_reads=[PP])
            ACT(XCB[c], XC[c], AF.Copy)

        def gates_tanh(c):
            Rb, Ab, Ib = LR[c]
            br = nbank()
            MM(br, (WS, wsl[:, c, :]), XCB[c], True, True)
            bi = nbank()
            MM(bi, (WS, wsl[:, 4 + c, :]), XCB[c], True, True)
            ACT(Rb, br, AF.Tanh, bias=der_t[:, dl + 8 + c:dl + 9 + c], scale=0.5, extra_reads=[DER])
            ACT(Ib, bi, AF.Tanh, bias=der_t[:, dl + 12 + c:dl + 13 + c], scale=0.5, extra_reads=[DER])

        def gate_piece(i, s):
            bks = win_piece(i, s)
            for o in range(2):
                ACT(MIX[2 * i + o], bks[o], AF.Gelu_apprx_tanh)

        def pool_adds(c):
            xb = XB[c]
            A_, B_ = PT[c]
            if c == 0:
                V("pool", (A_, A_.ap[:, 1:W]), (xb, xb.ap[:, 1:W]), (xb, xb.ap[:, 0:W - 1]), ALU.add)
                V("pool", (B_, B_.ap[64:128, 3:W]), (A_, A_.ap[64:128, 3:W]), (A_, A_.ap[64:128, 1:W - 2]), ALU.add)
            else:
                V("pool", (A_, A_.ap[:, 1:W]), (xb, xb.ap[:, 1:W]), (xb, xb.ap[:, 0:W - 1]), ALU.add)
                V("pool", (B_, B_.ap[:, 3:W]), (A_, A_.ap[:, 3:W]), (A_, A_.ap[:, 1:W - 2]), ALU.add)
                V("pool", (A_, A_.ap[:, 7:W]), (B_, B_.ap[:, 7:W]), (B_, B_.ap[:, 3:W - 4]), ALU.add)
                V("pool", (B_, B_.ap[64:128, 15:W]), (A_, A_.ap[64:128, 15:W]), (A_, A_.ap[64:128, 7:W - 8]), ALU.add)

        def pool_fin(c, first):
            xb = XB[c]
            A_, B_ = PT[c]
            for (p0, src) in ((0, A_), (64, B_)):
                STT((DB[c], DB[c].ap[p0:p0 + 64, :]), (src, src.ap[p0:p0 + 64, 16:W]),
                    cst_t[p0:p0 + 64, 128 + c:129 + c], (xb, xb.ap[p0:p0 + 64, 16:W]),
                    ALU.mult, ALU.subtract, extra_reads=[CST])
                if first:
                    V("dve", (FX, fx_t[p0:p0 + 64, :]), (src, src.ap[p0:p0 + 64, 16:32]),
                      (CST, cst_t[p0:p0 + 64, 130 + c * 16:130 + (c + 1) * 16]), ALU.mult)
                    V("dve", (DB[c], DB[c].ap[p0:p0 + 64, 0:16]), (FX, fx_t[p0:p0 + 64, :]),
                      (xb, xb.ap[p0:p0 + 64, 16:32]), ALU.subtract)

        def g_rest(s, dfr_dve, conv_done=False):
            first = (t == 0 and s == 0)
            cv = (lambda c: None) if conv_done else conv
            cv(0)
            pool_adds(0)
            gate_piece(0, s)
            gates_tanh(0)
            cv(1)
            if dfr_dve is not None:
                dfr_dve()
                dfr_dve = None
            pool_fin(0, first)
            gate_piece(1, s)
            gates_tanh(1)
            cv(2)
            pool_adds(1)
            bks = win_piece(5, s)
            for o in range(2):
                ACT(GU[o], bks[o], AF.Gelu_apprx_tanh)
            gates_tanh(2)
            cv(3)
            pool_fin(1, first)
            if dfr_dve is not None:
                dfr_dve()
            bks = win_piece(6, s)
            for o in range(2):
                ACT(GV[o], bks[o], AF.Gelu_apprx_tanh)
            gates_tanh(3)
            P.op("pool", lambda e: e.tensor_copy(out=hista_t[:, l, :, 1:4], in_=ar_tail_a()),
                 reads=XA, writes=[HISTA[l]])
            P.op("pool", lambda e: e.tensor_copy(out=histb_t[:, l, :, :], in_=ar_tail_b()),
                 reads=XB, writes=[HISTB[l]])
            for c in range(2):
                V("pool", VSQ[c], GV[c], GV[c], ALU.mult)
            bs_ = nbank()
            for c in range(2):
                MM(bs_, ONES, VSQ[c], c == 0, c == 1)
            for c in range(2):
                bp = nbank()
                MM(bp, (WS, wsl[:, 8 + c, :]), DB[c], True, True)
                TS("dve", MIX[4 + c], bp, ppc(base + 56 + c), None, ALU.mult, None, extra_reads=[PP])
            return bs_

        def b_phase(s, bs_, hooks):
            def lru_exp(c):
                Rb, Ab, Ib = LR[c]
                ACT(Ab, Rb, AF.Exp, scale=der_t[:, dl + c:dl + c + 1], bias=der_t[:, dl + c:dl + c + 1], extra_reads=[DER])
                STT(Rb, Ab, 0.99999994, Ab, ALU.min, ALU.mult)
                STT(Ib, Ib, 1.0, XC[c], ALU.add, ALU.mult)

            def lru_fin(c):
                Rb, Ab, Ib = LR[c]
                ACT(Rb, Rb, AF.Ln, bias=ONE_AP, scale=-1.0, extra_reads=[CONSTS])
                ACT(Rb, Rb, AF.Exp, bias=LNH_AP, scale=0.5, extra_reads=[CONSTS])
                V("dve", Ib, Ib, Rb, ALU.mult)
                P.op("dve", lambda e, Rb=Rb, Ab=Ab, Ib=Ib, c=c: e.tensor_tensor_scan(
                    out=Rb.ap, data0=Ab.ap, data1=Ib.ap, initial=hst_t[:, l, c:c + 1], op0=ALU.mult, op1=ALU.add),
                    reads=[Ab, Ib, HST[l]], writes=[Rb])
                P.op("dve", lambda e, Rb=Rb, c=c: e.tensor_copy(out=hst_t[:, l, c:c + 1], in_=Rb.ap[:, S - 1:S]),
                     reads=[Rb], writes=[HST[l]])
                V("dve", MIX[c], MIX[c], Rb, ALU.mult)

            hooks[0]()
            lru_exp(0)
            lru_exp(1)
            rv = RS[s]
            ACT(rv, bs_, AF.Ln, bias=EPS_AP, scale=1.0 / 256, extra_reads=[CONSTS])
            ACT(rv, rv, AF.Exp, scale=-0.5)
            for c in range(2):
                STT(VN[c], GV[c], ppc(base + 58 + c), rv, ALU.mult, ALU.mult, extra_reads=[PP])
            bt = nbank()
            btb = bt.ap.bitcast(BF16).rearrange("p (j d) -> p j d", j=4)
            for j in range(4):
                for c in range(2):
                    P.op("pe", lambda e, j=j, c=c, btb=btb: e.transpose(btb[:, j, c * 128:(c + 1) * 128],
                                                                        VN[c].ap[:, j * 128:(j + 1) * 128], ident_t[:]),
                         reads=[VN[c], IDENT], writes=[bt])
            P.op("dve", lambda e, btb=btb: e.tensor_copy(out=VTOK.ap, in_=btb), reads=[bt], writes=[VTOK])
            hooks[1]()
            lru_fin(0)
            lru_exp(2)
            lru_fin(1)
            lru_exp(3)
            for c in range(2):
                bz = nbank()
                for j in range(4):
                    for hh in range(2):
                        hd = 2 * c + hh
                        P.op("pe", lambda e, j=j, hh=hh, hd=hd, c=c, bz=bz: e.matmul(
                            bz.ap[hh * 64:(hh + 1) * 64, j * 128:(j + 1) * 128],
                            lhsT=VTOK.ap[:, j, c * 128 + hh * 64:c * 128 + (hh + 1) * 64],
                            rhs=wsl[:, 10 + hd, :], start=True, stop=True),
                            reads=[VTOK, WS], writes=[bz])
                bview = bias_t[:, (l * 2 + c) * 128:(l * 2 + c + 1) * 128].unsqueeze(1).broadcast_to([128, 4, 128])
                P.op("dve", lambda e, c=c, bz=bz, bview=bview: e.tensor_tensor(
                    out=ZT[c].ap.rearrange("p (j t) -> p j t", j=4), in0=bz.ap.rearrange("p (j t) -> p j t", j=4),
                    in1=bview, op=ALU.add), reads=[bz, BIAS], writes=[ZT[c]])
                V("pool", MIX[6 + c], ZT[c], GU[c], ALU.mult)
            hooks[2]()
            lru_fin(2)
            lru_fin(3)
            hooks[3]()
            if "a" not in branches:
                for c in range(4):
                    P.op("pool", lambda e, c=c: e.memset(MIX[c].ap, 0.0), writes=[MIX[c]])
            if "b" not in branches:
                for c in range(4, 6):
                    P.op("pool", lambda e, c=c: e.memset(MIX[c].ap, 0.0), writes=[MIX[c]])
            if "c" not in branches:
                for c in range(6, 8):
                    P.op("pool", lambda e, c=c: e.memset(MIX[c].ap, 0.0), writes=[MIX[c]])

        def w_phase(s, inter=None):
            acc = NormAcc(s)
            for i in range(4):
                slot = next_piece(("D", l, 0, i))
                w = ring_t[:, slot, :].rearrange("p (o k c) -> p o k c", o=2, k=KC)
                for o in range(2):
                    dch = 2 * i + o
                    by = nbank()
                    for k in range(KC):
                        MM(by, (RING[slot], w[:, o, k, :]), MIX[k], k == 0, k == KC - 1)
                    acc.pump(keep=1)
                    STT(X[dch][s], by, 1.0, X[dch][s], ALU.mult, ALU.add)
                    acc.feed(dch)
                    if inter is not None and dch % 2 == 1:
                        inter[dch // 2]()
            return acc

        nohook = lambda: None
        if deferred is not None:
            deferred("act")
            dfr_dve = lambda: deferred("dve")
        else:
            dfr_dve = None
        restore_hist()
        for i in (2, 3, 4):
            evac_front(i, win_piece(i, 0), "act")
        bs0 = g_rest(0, dfr_dve)
        if HOIST:
            parked = {}

            def h0():
                restore_hist()
                parked[2] = win_piece(2, 1)

            def h1():
                evac_front(2, parked[2], "act")
                parked[3] = win_piece(3, 1)

            def h2():
                evac_front(3, parked[3], "act")
                parked[4] = win_piece(4, 1)

            def h3():
                evac_front(4, parked[4], "act")
                conv(0)

            b_phase(0, bs0, [h0, h1, h2, h3])
        else:
            b_phase(0, bs0, [nohook] * 4)
        acc0 = w_phase(0, [(lambda: conv(1)), (lambda: conv(2)), (lambda: conv(3)), (lambda: None)] if HOIST else None)
        acc0.pump(0)
        acc0.finish_rest(next_gcol, next_dest, part="act")
        acc0_dve = lambda: acc0.finish_rest(next_gcol, next_dest, part="dve")
        if not HOIST:
            restore_hist()
            for i in (2, 3, 4):
                evac_front(i, win_piece(i, 1), "act")
        bs1 = g_rest(1, acc0_dve, conv_done=HOIST)
        b_phase(1, bs1, [nohook] * 4)
        acc1 = w_phase(1)
        acc1.pump(0)
        return lambda part="both": acc1.finish_rest(next_gcol, next_dest, part=part)

    xa_c0 = XA[0].ap.offset
    xa_stride = XA[1].ap.offset - XA[0].ap.offset
    assert XA[3].ap.offset == xa_c0 + 3 * xa_stride

    def ar_hist_a():
        return ar_t[:, xa_c0:xa_c0 + 4 * xa_stride].rearrange("p (c w) -> p c w", c=4)[:, :, 1:4]

    def ar_tail_a():
        return ar_t[:, xa_c0:xa_c0 + 4 * xa_stride].rearrange("p (c w) -> p c w", c=4)[:, :, S + 1:S + 4]

    xb_c0 = XB[0].ap.offset
    xb_stride = XB[1].ap.offset - XB[0].ap.offset

    def ar_hist_b():
        return ar_t[:, xb_c0:xb_c0 + 2 * xb_stride].rearrange("p (c w) -> p c w", c=2)[:, :, 0:16]

    def ar_tail_b():
        return ar_t[:, xb_c0:xb_c0 + 2 * xb_stride].rearrange("p (c w) -> p c w", c=2)[:, :, S:S + 16]

    id_d = nc.dram_tensor("ident", [128, 128], F32, kind="ExternalInput").ap()
    idf_t = sb("ident_f", [128, 128], F32)
    IDF = mk(idf_t[:], "identf")
    P.dma("sp", idf_t[:], id_d[:, :], misc[3], writes=[IDF])
    P.op("dve", lambda e: e.tensor_copy(out=ident_t[:], in_=idf_t[:]), reads=[IDF], writes=[IDENT])

    def gcol_of(st):
        l, kind = st
        return l * PL + {"f1": 0, "mx": 8, "f2": 16}[kind]

    def load_x_and_norm(t):
        for s in range(NS):
            for k in range(KC):
                P.dma("sp", X[k][s].ap, xT_d[k * 128:(k + 1) * 128, t * T + s * S:t * T + (s + 1) * S], x_ld[k][s],
                      writes=[X[k][s]])
        for s in range(NS):
            norm_standalone(s, gcol_of(stages[0]), H)

    last_is_ffn = stages[-1][1] != "mx"
    load_x_and_norm(0)
    for t in range(NT):
        dfr = None
        for idx, (l, kind) in enumerate(stages):
            is_last = idx + 1 == len(stages)
            if not is_last:
                ng, nd = gcol_of(stages[idx + 1]), H
            else:
                ng, nd = 2 * PL, YS
            if kind == "f1":
                dfr = ffn(l, 0, ng, nd, dfr, last=is_last)
            elif kind == "mx":
                dfr = mixer(l, t, ng, nd, dfr)
            else:
                dfr = ffn(l, 1, ng, nd, dfr, last=is_last)
        if last_is_ffn and t + 1 < NT:
            load_x_and_norm(t + 1)
            dfr()
        else:
            dfr()
            if t + 1 < NT:
                load_x_and_norm(t + 1)
        for s in range(NS):
            for k in range(KC):
                P.dma("sp", yT_d[k * 128:(k + 1) * 128, t * T + s * S:t * T + (s + 1) * S], YS[k][s].ap,
                      y_st[k][s], reads=[YS[k][s]])

    flush_casts(0)
    flush_stores(0)
    P.emit(final_waits=[(y_st[k][s], y_st[k][s].count) for k in range(KC) for s in range(NS)] + [(d, d.count) for d in ring_st if d.count > 0])
    es.close()
    stats = {e: len(P.ins[e]) for e in ENGS}
    stats["waits"] = P.n_wait
    return nc, stats


def _layout_weights(inp):
    f32 = np.float32

    def win_pieces(w):
        g = np.zeros((D, FC * 128), f32)
        u = np.zeros((D, FC * 128), f32)
        g[:, :DFF] = w[:, :DFF]
        u[:, :DFF] = w[:, DFF:]
        gu = np.stack([g, u], 0)
        gu = gu.reshape(2, KC, 128, FC, 128)
        return np.ascontiguousarray(gu.transpose(3, 2, 0, 1, 4)).reshape(FC, 128, PIECE)

    def wout_pieces(w):
        wp = np.zeros((FC * 128, D), f32)
        wp[:DFF] = w
        wp = wp.reshape(2, 11, 128, KC, 128)
        return np.ascontiguousarray(wp.transpose(3, 0, 2, 1, 4)).reshape(16, 128, WB_N)

    def pair_pieces(w, nout):
        wp = w.reshape(KC, 128, nout // 2, 2, 128)
        return np.ascontiguousarray(wp.transpose(2, 1, 3, 0, 4)).reshape(nout // 2, 128, PIECE)

    wA = np.stack([np.stack([win_pieces(inp["ffn1_w_in"][l]), win_pieces(inp["ffn2_w_in"][l])]) for l in range(L)])
    wB = np.stack([np.stack([wout_pieces(inp["ffn1_w_out"][l]), wout_pieces(inp["ffn2_w_out"][l])]) for l in range(L)])
    wC = np.stack([pair_pieces(inp["w_in"][l], 14) for l in range(L)])
    wD = np.stack([pair_pieces(inp["w_out"][l], 8) for l in range(L)])

    wS = np.zeros((L, 14, 128, 128), f32)
    for l in range(L):
        for c in range(4):
            for hh in range(2):
                h = 2 * c + hh
                sl = slice(hh * 64, (hh + 1) * 64)
                wS[l, c, sl, sl] = inp["rg_w_a"][l, h]
                wS[l, 4 + c, sl, sl] = inp["rg_w_x"][l, h]
        for c in range(2):
            for hh in range(2):
                g = 2 * c + hh
                sl = slice(hh * 64, (hh + 1) * 64)
                wS[l, 8 + c, sl, sl] = inp["pool_w"][l, g]
        for h in range(4):
            wS[l, 10 + h] = inp["sgu_w"][l, h].T
    wS = np.ascontiguousarray(wS.transpose(2, 0, 1, 3)).reshape(128, L * 14 * 128)

    pp = np.zeros((128, 2 * PL + 8), f32)

    def cols(v, n):
        return np.asarray(v, f32).reshape(n, 128).T

    for l in range(L):
        b = l * PL
        pp[:, b + 0:b + 8] = cols(inp["ffn1_norm"][l], 8)
        pp[:, b + 8:b + 16] = cols(inp["mix_norm"][l], 8)
        pp[:, b + 16:b + 24] = cols(inp["ffn2_norm"][l], 8)
        for tap in range(4):
            pp[:, b + 24 + tap * 4:b + 28 + tap * 4] = cols(inp["conv_w"][l, tap], 4)
        pp[:, b + 40:b + 44] = cols(inp["conv_b"][l], 4)
        pp[:, b + 44:b + 48] = cols(inp["rg_b_a"][l].reshape(-1), 4)
        pp[:, b + 48:b + 52] = cols(inp["rg_b_x"][l].reshape(-1), 4)
        pp[:, b + 52:b + 56] = cols(inp["lru_lambda"][l], 4)
        pp[:, b + 56:b + 58] = cols(inp["pool_scale"][l], 2)
        pp[:, b + 58:b + 60] = cols(inp["sgu_norm"][l], 2)
    pp[:, 2 * PL:2 * PL + 8] = cols(inp["final_norm"], 8)

    biasT = np.zeros((L, 2, 128, 128), f32)
    for l in range(L):
        for c in range(2):
            for hh in range(2):
                biasT[l, c, hh * 64:(hh + 1) * 64, :] = inp["sgu_b"][l, 2 * c + hh][None, :]
    biasT = np.ascontiguousarray(biasT.transpose(2, 0, 1, 3)).reshape(128, L * 2 * 128)

    cst = np.zeros((128, 128 + 2 + 32), f32)
    s_i = np.arange(128)[:, None]
    t_i = np.arange(128)[None, :]
    cst[:, 0:128] = (s_i <= t_i).astype(f32)
    wins = [(2, 4), (8, 16)]
    for c in range(2):
        wv = np.where(np.arange(128) < 64, wins[c][0], wins[c][1]).astype(f32)
        cst[:, 128 + c] = 1.0 / wv
        cnt = np.minimum(np.arange(16)[None, :] + 1, wv[:, None])
        cst[:, 130 + c * 16:130 + (c + 1) * 16] = 1.0 / cnt
    ident = np.eye(128, dtype=f32)
    return dict(wA=wA, wB=wB, wC=wC, wD=wD, wS=wS, pp=pp, biasT=biasT, cst=cst, ident=ident)


_CACHE = {}


def kernel(**inputs):
    inp = {k: np.asarray(v) for k, v in inputs.items()}
    x = inp["x"]
    B = x.shape[0]
    shared = _layout_weights(inp)
    if "nc" not in _CACHE:
        _CACHE["nc"] = build()[0]
    nc = _CACHE["nc"]
    in_maps = []
    for b in range(B):
        m = dict(shared)
        m["xT"] = np.ascontiguousarray(x[b].T)
        in_maps.append(m)
    res = run_bass_kernel_spmd(nc, in_maps, core_ids=list(range(B)))
    out = np.stack([np.ascontiguousarray(res.results[b]["yT"].T) for b in range(B)], 0)
    return out.astype(np.float32)
```

```python
import numpy as np
from contextlib import ExitStack
import concourse.bass as bass
import concourse.mybir as mybir
from concourse.bass_utils import run_bass_kernel_spmd

F32 = mybir.dt.float32
BF16 = mybir.dt.bfloat16
ALU = mybir.AluOpType
AF = mybir.ActivationFunctionType

D = 1024
KC = 8
SEQ = 8192
T = 1024
S = 512
NS = 2
L = 2
DFF = 2752
FC = 22
DIN = 1792
EPS = 1e-6
PL = 60
NB = 8
PIECE = 2048
WB_N = 1408

ENGS = ("pe", "act", "dve", "pool", "sp")


class Buf:
    __slots__ = ("name", "last_w", "readers")

    def __init__(self, name):
        self.name = name
        self.last_w = None
        self.readers = []


class Tl:
    __slots__ = ("ap", "bufs")

    def __init__(self, ap, bufs):
        self.ap = ap
        self.bufs = bufs


class DSem:
    def __init__(self, handle):
        self.h = handle
        self.count = 0


class Prog:
    def __init__(self, nc):
        self.nc = nc
        self.ins = {e: [] for e in ENGS}
        self.waited = {e: {} for e in ENGS}
        self.n_wait = 0

    def _add(self, eng, fn, reads, writes, dma_sem=None):
        rec = {"fn": fn, "waits": [], "mark": False, "dma_sem": dma_sem}
        idx = len(self.ins[eng])
        if dma_sem is not None:
            dma_sem.count += 16
            tok = ("dma", dma_sem, dma_sem.count)
        else:
            tok = ("eng", eng, idx)
        rb = []
        for t in reads:
            rb.extend(t.bufs)
        wbs = []
        for t in writes:
            wbs.extend(t.bufs)
        best = {}

        def consider(d):
            if d[0] == "eng":
                if d[1] == eng and dma_sem is None:
                    if eng == "pe":
                        return
                key = ("eng", d[1])
            else:
                key = ("dma", id(d[1]))
            v = d[2]
            if key not in best or best[key][1] < v:
                best[key] = (d, v)

        for b in rb:
            if b.last_w is not None:
                consider(b.last_w)
        for b in wbs:
            if b.last_w is not None:
                consider(b.last_w)
            for r in b.readers:
                consider(r)
        w = self.waited[eng]
        for key, (d, v) in best.items():
            if w.get(key, -1) >= v:
                continue
            w[key] = v
            rec["waits"].append(d)
            if d[0] == "eng":
                self.ins[d[1]][d[2]]["mark"] = True
        self.ins[eng].append(rec)
        for b in rb:
            b.readers.append(tok)
        for b in wbs:
            b.last_w = tok
            b.readers = []
        return tok

    def op(self, eng, fn, reads=(), writes=()):
        return self._add(eng, fn, list(reads), list(writes))

    def dma(self, eng, out, in_, dsem, reads=(), writes=()):
        def fn(e):
            return e.dma_start(out=out, in_=in_)
        return self._add(eng, fn, list(reads), list(writes), dma_sem=dsem)

    def emit(self, final_waits=()):
        nc = self.nc
        sems = {e: nc.alloc_semaphore(name=f"sem_{e}") for e in ENGS}
        for e in ENGS:
            c = 0
            for rec in self.ins[e]:
                if rec["mark"]:
                    c += 1
                    rec["val"] = c
        prog = self

        def run(eng_name, eng):
            for rec in prog.ins[eng_name]:
                for d in rec["waits"]:
                    if d[0] == "eng":
                        eng.wait_ge(sems[d[1]], prog.ins[d[1]][d[2]]["val"])
                    else:
                        eng.wait_ge(d[1].h, d[2])
                    prog.n_wait += 1
                inst = rec["fn"](eng)
                if rec["dma_sem"] is not None:
                    inst.then_inc(rec["dma_sem"].h, 16)
                elif rec["mark"]:
                    inst.then_inc(sems[eng_name], 1)
            if eng_name == "sp":
                for (ds, v) in final_waits:
                    eng.wait_ge(ds.h, v)

        with nc.Block() as block:
            @block.sync
            def _(e):
                run("sp", e)

            @block.tensor
            def _(e):
                run("pe", e)

            @block.scalar
            def _(e):
                run("act", e)

            @block.vector
            def _(e):
                run("dve", e)

            @block.gpsimd
            def _(e):
                run("pool", e)


def build(NT=SEQ // T, NL=L, nstage=None, branches="abc"):
    nc = bass.Bass("TRN2", target_bir_lowering=False)
    P = Prog(nc)
    es = ExitStack()
    ntok = NT * T

    xT_d = nc.dram_tensor("xT", [D, ntok], F32, kind="ExternalInput").ap()
    wA_d = nc.dram_tensor("wA", [L, 2, FC, 128, PIECE], F32, kind="ExternalInput").ap()
    wB_d = nc.dram_tensor("wB", [L, 2, 16, 128, WB_N], F32, kind="ExternalInput").ap()
    wC_d = nc.dram_tensor("wC", [L, 7, 128, PIECE], F32, kind="ExternalInput").ap()
    wD_d = nc.dram_tensor("wD", [L, 4, 128, PIECE], F32, kind="ExternalInput").ap()
    wS_d = nc.dram_tensor("wS", [128, L * 14 * 128], F32, kind="ExternalInput").ap()
    pp_d = nc.dram_tensor("pp", [128, 2 * PL + 8], F32, kind="ExternalInput").ap()
    bias_d = nc.dram_tensor("biasT", [128, L * 2 * 128], F32, kind="ExternalInput").ap()
    cst_d = nc.dram_tensor("cst", [128, 128 + 2 + 32], F32, kind="ExternalInput").ap()
    yT_d = nc.dram_tensor("yT", [D, ntok], F32, kind="ExternalOutput").ap()
    sA_d = nc.dram_tensor("sA", [L, 2, FC, 128, PIECE], BF16, kind="Internal").ap()
    sB_d = nc.dram_tensor("sB", [L, 2, 16, 128, WB_N], BF16, kind="Internal").ap()
    sC_d = nc.dram_tensor("sC", [L, 7, 128, PIECE], BF16, kind="Internal").ap()
    sD_d = nc.dram_tensor("sD", [L, 4, 128, PIECE], BF16, kind="Internal").ap()

    def sb(name, shape, dt):
        return es.enter_context(nc.sbuf_tensor(name, shape, dt))

    def newsem(name):
        return DSem(nc.alloc_semaphore(name=name))

    def mk(ap, name):
        return Tl(ap, [Buf(name)])

    x_t = sb("x_res", [128, KC, T], F32)
    X = [[mk(x_t[:, k, s * S:(s + 1) * S], f"x{k}_{s}") for s in range(NS)] for k in range(KC)]
    h_t = sb("h_bf", [128, KC, T], BF16)
    H = [[mk(h_t[:, k, s * S:(s + 1) * S], f"h{k}_{s}") for s in range(NS)] for k in range(KC)]
    ring_t = sb("ring", [128, NB, PIECE], BF16)
    RING = [mk(ring_t[:, i, :], f"ring{i}") for i in range(NB)]
    stg_t = sb("stg", [128, 2, PIECE], F32)
    STG = [mk(stg_t[:, i, :], f"stg{i}") for i in range(2)]
    ws_t = sb("ws_bf", [128, L * 14 * 128], BF16)
    WS = mk(ws_t[:], "ws")
    pp_t = sb("pp_sb", [128, 2 * PL + 8], F32)
    PP = mk(pp_t[:], "pp")
    der_t = sb("der", [128, L * 16], F32)
    DER = mk(der_t[:], "der")
    bias_t = sb("bias_sb", [128, L * 2 * 128], F32)
    BIAS = mk(bias_t[:], "bias")
    cst_t = sb("cst_sb", [128, 128 + 2 + 32], F32)
    CST = mk(cst_t[:], "cst")
    ident_t = sb("ident_bf", [128, 128], BF16)
    IDENT = mk(ident_t[:], "ident")
    ones_t = sb("ones_bf", [128, 128], BF16)
    ONES = mk(ones_t[:], "ones")
    xsq_t = sb("xsq", [128, KC, S], BF16)
    XSQ = [mk(xsq_t[:, k, :], f"xsq{k}") for k in range(KC)]
    rs_t = sb("rs", [128, 2, S], F32)
    RS = [mk(rs_t[:, i, :], f"rs{i}") for i in range(2)]
    hista_t = sb("hista", [128, L, 4, 4], F32)
    HISTA = [mk(hista_t[:, l], f"hista{l}") for l in range(L)]
    histb_t = sb("histb", [128, L, 2, 16], F32)
    HISTB = [mk(histb_t[:, l], f"histb{l}") for l in range(L)]
    hst_t = sb("hstate", [128, L, 4], F32)
    HST = [mk(hst_t[:, l], f"hst{l}") for l in range(L)]

    ARENA_KB = 84
    ar_t = sb("arena", [128, ARENA_KB * 256], F32)
    ar_bufs = [Buf(f"ar{i}") for i in range(ARENA_KB)]

    def carve(off_b, shape_free, dt):
        esz = 4 if dt == F32 else 2
        n = int(np.prod(shape_free))
        nbytes = n * esz
        assert off_b % 4 == 0
        c0 = off_b // 4
        ncol32 = (nbytes + 3) // 4
        ap = ar_t[:, c0:c0 + ncol32]
        if dt != F32:
            ap = ap.bitcast(dt)[:, 0:n]
        if len(shape_free) == 2:
            ap = ap.rearrange("p (a b) -> p a b", a=shape_free[0])
        b0 = off_b // 1024
        b1 = (off_b + nbytes - 1) // 1024
        assert b1 < ARENA_KB, (off_b, nbytes)
        return Tl(ap, ar_bufs[b0:b1 + 1])

    class Alloc:
        def __init__(self, base=0):
            self.off = base

        def get(self, shape_free, dt):
            t = carve(self.off, shape_free, dt)
            esz = 4 if dt == F32 else 2
            nb = int(np.prod(shape_free)) * esz
            self.off += ((nb + 1023) // 1024) * 1024
            return t

    a1 = Alloc(0)
    SG = [a1.get([S], F32) for _ in range(2)]
    AT = [[a1.get([S], BF16) for s in range(NS)] for f in range(FC)]
    assert a1.off <= 48 * 1024
    a2 = Alloc(48 * 1024)
    YS = [[a2.get([S], F32) for s in range(NS)] for k in range(KC)]
    assert a2.off <= ARENA_KB * 1024
    a3 = Alloc(0)
    XA = [a3.get([4 + S], F32) for c in range(4)]
    XC = [a3.get([S], F32) for c in range(4)]
    XCB = [a3.get([S], BF16) for c in range(4)]
    LR = [[a3.get([S], F32) for j in range(3)] for c in range(4)]
    XB = [a3.get([16 + S], F32) for c in range(2)]
    _pt = [a3.get([16 + S], F32) for j in range(2)]
    PT = [_pt, _pt]
    DB = [a3.get([S], BF16) for c in range(2)]
    GU = [a3.get([S], F32) for c in range(2)]
    GV = [a3.get([S], F32) for c in range(2)]
    VSQ = [a3.get([S], BF16) for c in range(2)]
    VN = [a3.get([S], BF16) for c in range(2)]
    VTOK = a3.get([4, 256], BF16)
    ZT = GV
    MIX = [a3.get([S], BF16) for c in range(8)]
    assert a3.off <= ARENA_KB * 1024, a3.off

    banks = []
    for i in range(8):
        pt_ = es.enter_context(nc.psum_tensor(f"bank{i}", [128, S], F32))
        banks.append(mk(pt_[:], f"bank{i}"))
    bank_ctr = [0]

    ring_ld = [newsem(f"rl{i}") for i in range(NB)]
    ring_st = [newsem(f"rs{i}") for i in range(NB)]
    stg_ld = [newsem(f"sl{i}") for i in range(2)]
    x_ld = [[newsem(f"xl{k}_{s}") for s in range(NS)] for k in range(KC)]
    y_st = [[newsem(f"ys{k}_{s}") for s in range(NS)] for k in range(KC)]
    misc = [newsem(f"ms{i}") for i in range(5)]

    def V(eng, out, in0, in1, op, reads=None, writes=None):
        def g(t):
            return (t, t.ap) if isinstance(t, Tl) else t
        (to, ao), (t0, a0), (t1, a1_) = g(out), g(in0), g(in1)
        P.op(eng, lambda e: e.tensor_tensor(out=ao, in0=a0, in1=a1_, op=op),
             reads=[t0, t1], writes=[to])

    def TS(eng, out, in0, s1, s2, op0, op1, extra_reads=()):
        def g(t):
            return (t, t.ap) if isinstance(t, Tl) else t
        (to, ao), (t0, a0) = g(out), g(in0)
        if op1 is None:
            P.op(eng, lambda e: e.tensor_scalar(out=ao, in0=a0, scalar1=s1, scalar2=None, op0=op0),
                 reads=[t0] + list(extra_reads), writes=[to])
        else:
            P.op(eng, lambda e: e.tensor_scalar(out=ao, in0=a0, scalar1=s1, scalar2=s2, op0=op0, op1=op1),
                 reads=[t0] + list(extra_reads), writes=[to])

    def STT(out, in0, scalar, in1, op0, op1, extra_reads=()):
        def g(t):
            return (t, t.ap) if isinstance(t, Tl) else t
        (to, ao), (t0, a0), (t1, a1_) = g(out), g(in0), g(in1)
        P.op("dve", lambda e: e.scalar_tensor_tensor(out=ao, in0=a0, scalar=scalar, in1=a1_, op0=op0, op1=op1),
             reads=[t0, t1] + list(extra_reads), writes=[to])

    def ACT(out, in_, func, bias=None, scale=None, extra_reads=()):
        def g(t):
            return (t, t.ap) if isinstance(t, Tl) else t
        (to, ao), (t0, a0) = g(out), g(in_)
        kw = {}
        if bias is not None:
            kw["bias"] = bias
        if scale is not None:
            kw["scale"] = scale
        P.op("act", lambda e: e.activation(out=ao, in_=a0, func=func, **kw),
             reads=[t0] + list(extra_reads), writes=[to])

    def CP(eng, out, in_):
        def g(t):
            return (t, t.ap) if isinstance(t, Tl) else t
        (to, ao), (t0, a0) = g(out), g(in_)
        P.op(eng, lambda e: e.tensor_copy(out=ao, in_=a0), reads=[t0], writes=[to])

    def MM(out, lhsT, rhs, start, stop):
        def g(t):
            return (t, t.ap) if isinstance(t, Tl) else t
        (to, ao), (tl, al), (tr, ar) = g(out), g(lhsT), g(rhs)
        P.op("pe", lambda e: e.matmul(ao, lhsT=al, rhs=ar, start=start, stop=stop),
             reads=[tl, tr], writes=[to])

    def ppc(col):
        return pp_t[:, col:col + 1]

    P.dma("sp", pp_t[:], pp_d[:, :], misc[0], writes=[PP])
    P.dma("sp", cst_t[:], cst_d[:, :], misc[1], writes=[CST])
    P.dma("sp", bias_t[:], bias_d[:, :], misc[2], writes=[BIAS])
    nws = L * 14 * 128
    half = nws // 2
    for hh in range(2):
        P.dma("sp", stg_t[:, hh, 0:half], wS_d[:, hh * half:(hh + 1) * half], stg_ld[hh], writes=[STG[hh]])
    for l in range(L):
        for hd in range(4):
            o = (10 + hd) * 128
            P.op("dve", lambda e, l=l, o=o: e.tensor_tensor(out=stg_t[:, l, o:o + 128], in0=stg_t[:, l, o:o + 128],
                                                            in1=cst_t[:, 0:128], op=ALU.mult),
                 reads=[STG[l], CST], writes=[STG[l]])
        P.op("dve", lambda e, l=l: e.tensor_copy(out=ws_t[:, l * half:(l + 1) * half], in_=stg_t[:, l, 0:half]),
             reads=[STG[l]], writes=[WS])
    P.op("dve", lambda e: e.memset(ones_t[:], 1.0), writes=[ONES])
    for l in range(L):
        lam = pp_t[:, l * PL + 52:l * PL + 56]
        d0 = l * 16
        P.op("act", lambda e, d0=d0, lam=lam: e.activation(out=der_t[:, d0:d0 + 4], in_=lam, func=AF.Sigmoid),
             reads=[PP], writes=[DER])
        P.op("act", lambda e, d0=d0: e.activation(out=der_t[:, d0 + 4:d0 + 8], in_=der_t[:, d0:d0 + 4], func=AF.Ln),
             reads=[DER], writes=[DER])
        P.op("dve", lambda e, d0=d0: e.tensor_scalar(out=der_t[:, d0:d0 + 4], in0=der_t[:, d0 + 4:d0 + 8],
                                                     scalar1=4.0, scalar2=None, op0=ALU.mult),
             reads=[DER], writes=[DER])
        P.op("dve", lambda e, d0=d0: e.tensor_scalar(out=der_t[:, d0 + 4:d0 + 8], in0=der_t[:, d0 + 4:d0 + 8],
                                                     scalar1=8.0, scalar2=None, op0=ALU.mult),
             reads=[DER], writes=[DER])
        P.op("dve", lambda e, d0=d0, l=l: e.tensor_scalar(out=der_t[:, d0 + 8:d0 + 16], in0=pp_t[:, l * PL + 44:l * PL + 52],
                                                          scalar1=0.5, scalar2=None, op0=ALU.mult),
             reads=[PP], writes=[DER])
        P.op("pool", lambda e, l=l: e.memset(hista_t[:, l], 0.0), writes=[HISTA[l]])
        P.op("pool", lambda e, l=l: e.memset(histb_t[:, l], 0.0), writes=[HISTB[l]])
        P.op("pool", lambda e, l=l: e.memset(hst_t[:, l], 0.0), writes=[HST[l]])

    converted = set()
    state = {"issued": 0, "stg": 0}
    plan = []

    def piece_aps(pc):
        kind, l, fi, i = pc
        if kind == "A":
            return wA_d[l, fi, i], sA_d[l, fi, i], PIECE
        if kind == "B":
            return wB_d[l, fi, i], sB_d[l, fi, i], WB_N
        if kind == "C":
            return wC_d[l, i], sC_d[l, i], PIECE
        return wD_d[l, i], sD_d[l, i], PIECE

    scratch_bufs = {}

    pending_st = []
    pending_cast = []

    def flush_stores(keep=0):
        while len(pending_st) > keep:
            (scr, slot, n, sbuf) = pending_st.pop(0)
            P.dma("sp", scr, ring_t[:, slot, 0:n], ring_st[slot], reads=[RING[slot]], writes=[sbuf])

    def flush_casts(keep=0):
        while len(pending_cast) > keep:
            (q, slot, n, scr, sbuf) = pending_cast.pop(0)
            P.op("act", lambda e, q=q, slot=slot, n=n: e.activation(out=ring_t[:, slot, 0:n], in_=stg_t[:, q, 0:n], func=AF.Copy),
                 reads=[STG[q]], writes=[RING[slot]])
            pending_st.append((scr, slot, n, sbuf))

    def issue_one():
        j = state["issued"]
        pc = plan[j]
        slot = j % NB
        src, scr, n = piece_aps(pc)
        if pc not in scratch_bufs:
            scratch_bufs[pc] = Tl(None, [Buf(f"scr{pc}")])
        sbuf = scratch_bufs[pc]
        flush_stores(0)
        flush_casts(0)
        if pc not in converted:
            converted.add(pc)
            q = state["stg"] % 2
            state["stg"] += 1
            P.dma("sp", stg_t[:, q, 0:n], src, stg_ld[q], writes=[STG[q]])
            pending_cast.append((q, slot, n, scr, sbuf))
        else:
            P.dma("sp", ring_t[:, slot, 0:n], scr, ring_ld[slot], reads=[sbuf], writes=[RING[slot]])
        state["issued"] += 1

    use_ctr = [0]

    def next_piece(expect):
        j = use_ctr[0]
        assert plan[j] == expect, (plan[j], expect)
        while state["issued"] < min(len(plan), j + NB - 1):
            issue_one()
        if state["issued"] <= j + 1:
            flush_casts(0)
        use_ctr[0] += 1
        return j % NB

    stages = []
    for l in range(NL):
        stages += [(l, "f1"), (l, "mx"), (l, "f2")]
    if nstage is not None:
        stages = stages[:nstage]
    for t in range(NT):
        for (l, kind) in stages:
            if kind == "mx":
                for i in (2, 3, 4, 0, 1, 5, 6):
                    plan.append(("C", l, 0, i))
                for i in (2, 3, 4):
                    plan.append(("C", l, 0, i))
                for i in range(4):
                    plan.append(("D", l, 0, i))
                for i in (0, 1, 5, 6):
                    plan.append(("C", l, 0, i))
                for i in range(4):
                    plan.append(("D", l, 0, i))
            else:
                fi = 0 if kind == "f1" else 1
                for f in range(FC):
                    plan.append(("A", l, fi, f))
                for i in range(16):
                    plan.append(("B", l, fi, i))

    held = set()

    def nbank():
        while True:
            i = bank_ctr[0] % 8
            bank_ctr[0] += 1
            if i not in held:
                return banks[i]

    def hold_bank():
        b = nbank()
        held.add(banks.index(b))
        return b

    def release_bank(b):
        held.discard(banks.index(b))

    eps_t = sb("eps_sb", [128, 4], F32)
    CONSTS = mk(eps_t[:], "consts")
    P.op("pool", lambda e: e.memset(eps_t[:, 0:1], EPS), writes=[CONSTS])
    P.op("pool", lambda e: e.memset(eps_t[:, 1:2], 1.0), writes=[CONSTS])
    P.op("pool", lambda e: e.memset(eps_t[:, 2:3], -0.6931471805599453), writes=[CONSTS])
    EPS_AP = eps_t[:, 0:1]
    ONE_AP = eps_t[:, 1:2]
    LNH_AP = eps_t[:, 2:3]
    fx_t = sb("fx_sb", [128, 16], F32)
    FX = mk(fx_t[:], "fx")

    rs2_t = sb("rs2", [128, 2, S], F32)
    RS2 = [mk(rs2_t[:, i, :], f"rs2_{i}") for i in range(2)]

    def norm_finish(s, st, gcol, dest, dim, src=None, r=None, part="both"):
        src = X if src is None else src
        r = RS[s] if r is None else r
        if part in ("both", "act"):
            ACT(r, st, AF.Ln, bias=EPS_AP, scale=1.0 / dim, extra_reads=[CONSTS])
            ACT(r, r, AF.Exp, scale=-0.5)
        if part in ("both", "dve"):
            for k in range(KC):
                STT(dest[k][s], src[k][s], ppc(gcol + k), r, ALU.mult, ALU.mult, extra_reads=[PP])

    sq_ctr = [0]

    class NormAcc:
        def __init__(self, s, src=None):
            self.s = s
            self.src = X if src is None else src
            self.st = hold_bank()
            self.pending = []
            self.n = 0

        def feed(self, dch, eng="pool"):
            q = XSQ[sq_ctr[0] % KC]
            sq_ctr[0] += 1
            V(eng, q, self.src[dch][self.s], self.src[dch][self.s], ALU.mult)
            self.pending.append(q)

        def pump(self, keep=0):
            while len(self.pending) > keep:
                q = self.pending.pop(0)
                MM(self.st, ONES, q, self.n == 0, self.n == KC - 1)
                self.n += 1

        def finish(self, gcol, dest):
            self.pump(0)
            self.finish_rest(gcol, dest)

        def finish_rest(self, gcol, dest, r=None, part="both"):
            assert self.n == KC and not self.pending
            norm_finish(self.s, self.st, gcol, dest, D, src=self.src, r=r, part=part)
            if part in ("both", "act"):
                release_bank(self.st)

    def norm_standalone(s, gcol, dest):
        st = nbank()
        for k in range(KC):
            eng = "pool" if (s == 1 or k % 8 in (3, 6, 7)) else "dve"
            V(eng, XSQ[k], X[k][s], X[k][s], ALU.mult)
        for k in range(KC):
            MM(st, ONES, XSQ[k], k == 0, k == KC - 1)
        norm_finish(s, st, gcol, dest, D)

    def ffn(l, fi, next_gcol, next_dest, deferred=None, last=False):
        if deferred is not None:
            deferred("act")
        for f in range(FC):
            slot = next_piece(("A", l, fi, f))
            w = ring_t[:, slot, :].rearrange("p (g k c) -> p g k c", g=2, k=KC)
            for s in range(NS):
                bg = nbank()
                bu = nbank()
                for k in range(KC):
                    MM(bg, (RING[slot], w[:, 0, k, :]), H[k][s], k == 0, k == KC - 1)
                for k in range(KC):
                    MM(bu, (RING[slot], w[:, 1, k, :]), H[k][s], k == 0, k == KC - 1)
                if deferred is not None:
                    deferred("dve")
                    deferred = None
                sg = SG[(f * NS + s) % 2]
                ACT(sg, bg, AF.Silu)
                V("dve", AT[f][s], sg, bu, ALU.mult)
        xdst = YS if last else X
        accs = [NormAcc(s, src=xdst) for s in range(NS)]
        for dch in range(KC):
            slots = []
            for hf in range(2):
                slots.append(next_piece(("B", l, fi, dch * 2 + hf)))
            for s in range(NS):
                by = nbank()
                for hf in range(2):
                    slot = slots[hf]
                    w = ring_t[:, slot, 0:WB_N].rearrange("p (j c) -> p j c", j=11)
                    for j in range(11):
                        f = hf * 11 + j
                        MM(by, (RING[slot], w[:, j, :]), AT[f][s], f == 0, f == FC - 1)
                accs[0].pump()
                accs[1].pump()
                STT(xdst[dch][s], by, 0.5, X[dch][s], ALU.mult, ALU.add)
                accs[s].feed(dch)
        if last:
            for s in range(NS):
                accs[s].pump(0)
                accs[s].finish_rest(next_gcol, next_dest, r=RS2[s], part="act")
            return lambda part="both": ([accs[s].finish_rest(next_gcol, next_dest, r=RS2[s], part="dve") for s in range(NS)]
                                        if part != "act" else None)
        accs[0].finish(next_gcol, next_dest)
        accs[1].pump(0)
        return lambda part="both": accs[1].finish_rest(next_gcol, next_dest, part=part)

    C_ORDER = (2, 3, 4, 5, 6, 0, 1)

    HOIST = True

    def mixer(l, t, next_gcol, next_dest, deferred=None):
        base = l * PL
        dl = l * 16
        wsl = ws_t[:, l * half:(l + 1) * half].rearrange("p (m c) -> p m c", m=14)
        W = 16 + S

        def win_piece(i, s):
            slot = next_piece(("C", l, 0, i))
            w = ring_t[:, slot, :].rearrange("p (o k c) -> p o k c", o=2, k=KC)
            outs = []
            for o in range(2):
                bk = nbank()
                for k in range(KC):
                    MM(bk, (RING[slot], w[:, o, k, :]), H[k][s], k == 0, k == KC - 1)
                outs.append(bk)
            return outs

        def restore_hist():
            P.op("pool", lambda e: e.tensor_copy(out=ar_hist_a(), in_=hista_t[:, l, :, 1:4]),
                 reads=[HISTA[l]], writes=XA)
            P.op("pool", lambda e: e.tensor_copy(out=ar_hist_b(), in_=histb_t[:, l, :, :]),
                 reads=[HISTB[l]], writes=XB)

        def evac_front(i, bks, eng):
            for o in range(2):
                if i in (2, 3):
                    c = 2 * i + o - 4
                    dst = (XA[c], XA[c].ap[:, 4:4 + S])
                else:
                    dst = (XB[o], XB[o].ap[:, 16:W])
                if eng == "act":
                    ACT(dst, bks[o], AF.Identity)
                else:
                    CP("dve", dst, bks[o])

        def conv(c):
            xa = XA[c]
            cw = lambda tap: ppc(base + 24 + tap * 4 + c)
            TS("dve", XC[c], (xa, xa.ap[:, 4:4 + S]), cw(3), ppc(base + 40 + c), ALU.mult, ALU.add, extra_reads=[PP])
            for tap in (2, 1, 0):
                sh = 3 - tap
                STT(XC[c], (xa, xa.ap[:, 4 - sh:4 - sh + S]), cw(tap), XC[c], ALU.mult, ALU.add, extra_reads=[PP])
            ACT(XCB[c], XC[c], AF.Copy)

        def gates_tanh(c):
            Rb, Ab, Ib = LR[c]
            br = nbank()
            MM(br, (WS, wsl[:, c, :]), XCB[c], True, True)
            bi = nbank()
            MM(bi, (WS, wsl[:, 4 + c, :]), XCB[c], True, True)
            ACT(Rb, br, AF.Tanh, bias=der_t[:, dl + 8 + c:dl + 9 + c], scale=0.5, extra_reads=[DER])
            ACT(Ib, bi, AF.Tanh, bias=der_t[:, dl + 12 + c:dl + 13 + c], scale=0.5, extra_reads=[DER])

        def gate_piece(i, s):
            bks = win_piece(i, s)
            for o in range(2):
                ACT(MIX[2 * i + o], bks[o], AF.Gelu_apprx_tanh)

        def pool_adds(c):
            xb = XB[c]
            A_, B_ = PT[c]
            if c == 0:
                V("pool", (A_, A_.ap[:, 1:W]), (xb, xb.ap[:, 1:W]), (xb, xb.ap[:, 0:W - 1]), ALU.add)
                V("pool", (B_, B_.ap[64:128, 3:W]), (A_, A_.ap[64:128, 3:W]), (A_, A_.ap[64:128, 1:W - 2]), ALU.add)
            else:
                V("pool", (A_, A_.ap[:, 1:W]), (xb, xb.ap[:, 1:W]), (xb, xb.ap[:, 0:W - 1]), ALU.add)
                V("pool", (B_, B_.ap[:, 3:W]), (A_, A_.ap[:, 3:W]), (A_, A_.ap[:, 1:W - 2]), ALU.add)
                V("pool", (A_, A_.ap[:, 7:W]), (B_, B_.ap[:, 7:W]), (B_, B_.ap[:, 3:W - 4]), ALU.add)
                V("pool", (B_, B_.ap[64:128, 15:W]), (A_, A_.ap[64:128, 15:W]), (A_, A_.ap[64:128, 7:W - 8]), ALU.add)

        def pool_fin(c, first):
            xb = XB[c]
            A_, B_ = PT[c]
            for (p0, src) in ((0, A_), (64, B_)):
                STT((DB[c], DB[c].ap[p0:p0 + 64, :]), (src, src.ap[p0:p0 + 64, 16:W]),
                    cst_t[p0:p0 + 64, 128 + c:129 + c], (xb, xb.ap[p0:p0 + 64, 16:W]),
                    ALU.mult, ALU.subtract, extra_reads=[CST])
                if first:
                    V("dve", (FX, fx_t[p0:p0 + 64, :]), (src, src.ap[p0:p0 + 64, 16:32]),
                      (CST, cst_t[p0:p0 + 64, 130 + c * 16:130 + (c + 1) * 16]), ALU.mult)
                    V("dve", (DB[c], DB[c].ap[p0:p0 + 64, 0:16]), (FX, fx_t[p0:p0 + 64, :]),
                      (xb, xb.ap[p0:p0 + 64, 16:32]), ALU.subtract)

        def g_rest(s, dfr_dve, conv_done=False):
            first = (t == 0 and s == 0)
            cv = (lambda c: None) if conv_done else conv
            cv(0)
            pool_adds(0)
            gate_piece(0, s)
            gates_tanh(0)
            cv(1)
            if dfr_dve is not None:
                dfr_dve()
                dfr_dve = None
            pool_fin(0, first)
            gate_piece(1, s)
            gates_tanh(1)
            cv(2)
            pool_adds(1)
            bks = win_piece(5, s)
            for o in range(2):
                ACT(GU[o], bks[o], AF.Gelu_apprx_tanh)
            gates_tanh(2)
            cv(3)
            pool_fin(1, first)
            if dfr_dve is not None:
                dfr_dve()
            bks = win_piece(6, s)
            for o in range(2):
                ACT(GV[o], bks[o], AF.Gelu_apprx_tanh)
            gates_tanh(3)
            P.op("pool", lambda e: e.tensor_copy(out=hista_t[:, l, :, 1:4], in_=ar_tail_a()),
                 reads=XA, writes=[HISTA[l]])
            P.op("pool", lambda e: e.tensor_copy(out=histb_t[:, l, :, :], in_=ar_tail_b()),
                 reads=XB, writes=[HISTB[l]])
            for c in range(2):
                V("pool", VSQ[c], GV[c], GV[c], ALU.mult)
            bs_ = nbank()
            for c in range(2):
                MM(bs_, ONES, VSQ[c], c == 0, c == 1)
            for c in range(2):
                bp = nbank()
                MM(bp, (WS, wsl[:, 8 + c, :]), DB[c], True, True)
                TS("dve", MIX[4 + c], bp, ppc(base + 56 + c), None, ALU.mult, None, extra_reads=[PP])
            return bs_

        def b_phase(s, bs_, hooks):
            def lru_exp(c):
                Rb, Ab, Ib = LR[c]
                ACT(Ab, Rb, AF.Exp, scale=der_t[:, dl + c:dl + c + 1], bias=der_t[:, dl + c:dl + c + 1], extra_reads=[DER])
                STT(Rb, Ab, 0.99999994, Ab, ALU.min, ALU.mult)
                STT(Ib, Ib, 1.0, XC[c], ALU.add, ALU.mult)

            def lru_fin(c):
                Rb, Ab, Ib = LR[c]
                ACT(Rb, Rb, AF.Ln, bias=ONE_AP, scale=-1.0, extra_reads=[CONSTS])
                ACT(Rb, Rb, AF.Exp, bias=LNH_AP, scale=0.5, extra_reads=[CONSTS])
                V("dve", Ib, Ib, Rb, ALU.mult)
                P.op("dve", lambda e, Rb=Rb, Ab=Ab, Ib=Ib, c=c: e.tensor_tensor_scan(
                    out=Rb.ap, data0=Ab.ap, data1=Ib.ap, initial=hst_t[:, l, c:c + 1], op0=ALU.mult, op1=ALU.add),
                    reads=[Ab, Ib, HST[l]], writes=[Rb])
                P.op("dve", lambda e, Rb=Rb, c=c: e.tensor_copy(out=hst_t[:, l, c:c + 1], in_=Rb.ap[:, S - 1:S]),
                     reads=[Rb], writes=[HST[l]])
                V("dve", MIX[c], MIX[c], Rb, ALU.mult)

            hooks[0]()
            lru_exp(0)
            lru_exp(1)
            rv = RS[s]
            ACT(rv, bs_, AF.Ln, bias=EPS_AP, scale=1.0 / 256, extra_reads=[CONSTS])
            ACT(rv, rv, AF.Exp, scale=-0.5)
            for c in range(2):
                STT(VN[c], GV[c], ppc(base + 58 + c), rv, ALU.mult, ALU.mult, extra_reads=[PP])
            bt = nbank()
            btb = bt.ap.bitcast(BF16).rearrange("p (j d) -> p j d", j=4)
            for j in range(4):
                for c in range(2):
                    P.op("pe", lambda e, j=j, c=c, btb=btb: e.transpose(btb[:, j, c * 128:(c + 1) * 128],
                                                                        VN[c].ap[:, j * 128:(j + 1) * 128], ident_t[:]),
                         reads=[VN[c], IDENT], writes=[bt])
            P.op("dve", lambda e, btb=btb: e.tensor_copy(out=VTOK.ap, in_=btb), reads=[bt], writes=[VTOK])
            hooks[1]()
            lru_fin(0)
            lru_exp(2)
            lru_fin(1)
            lru_exp(3)
            for c in range(2):
                bz = nbank()
                for j in range(4):
                    for hh in range(2):
                        hd = 2 * c + hh
                        P.op("pe", lambda e, j=j, hh=hh, hd=hd, c=c, bz=bz: e.matmul(
                            bz.ap[hh * 64:(hh + 1) * 64, j * 128:(j + 1) * 128],
                            lhsT=VTOK.ap[:, j, c * 128 + hh * 64:c * 128 + (hh + 1) * 64],
                            rhs=wsl[:, 10 + hd, :], start=True, stop=True),
                            reads=[VTOK, WS], writes=[bz])
                bview = bias_t[:, (l * 2 + c) * 128:(l * 2 + c + 1) * 128].unsqueeze(1).broadcast_to([128, 4, 128])
                P.op("dve", lambda e, c=c, bz=bz, bview=bview: e.tensor_tensor(
                    out=ZT[c].ap.rearrange("p (j t) -> p j t", j=4), in0=bz.ap.rearrange("p (j t) -> p j t", j=4),
                    in1=bview, op=ALU.add), reads=[bz, BIAS], writes=[ZT[c]])
                V("pool", MIX[6 + c], ZT[c], GU[c], ALU.mult)
            hooks[2]()
            lru_fin(2)
            lru_fin(3)
            hooks[3]()
            if "a" not in branches:
                for c in range(4):
                    P.op("pool", lambda e, c=c: e.memset(MIX[c].ap, 0.0), writes=[MIX[c]])
            if "b" not in branches:
                for c in range(4, 6):
                    P.op("pool", lambda e, c=c: e.memset(MIX[c].ap, 0.0), writes=[MIX[c]])
            if "c" not in branches:
                for c in range(6, 8):
                    P.op("pool", lambda e, c=c: e.memset(MIX[c].ap, 0.0), writes=[MIX[c]])

        def w_phase(s, inter=None):
            acc = NormAcc(s)
            for i in range(4):
                slot = next_piece(("D", l, 0, i))
                w = ring_t[:, slot, :].rearrange("p (o k c) -> p o k c", o=2, k=KC)
                for o in range(2):
                    dch = 2 * i + o
                    by = nbank()
                    for k in range(KC):
                        MM(by, (RING[slot], w[:, o, k, :]), MIX[k], k == 0, k == KC - 1)
                    acc.pump(keep=2)
                    STT(X[dch][s], by, 1.0, X[dch][s], ALU.mult, ALU.add)
                    acc.feed(dch)
                    if inter is not None and dch % 2 == 1:
                        inter[dch // 2]()
            return acc

        nohook = lambda: None
        if deferred is not None:
            deferred("act")
            dfr_dve = lambda: deferred("dve")
        else:
            dfr_dve = None
        restore_hist()
        for i in (2, 3, 4):
            evac_front(i, win_piece(i, 0), "act")
        bs0 = g_rest(0, dfr_dve)
        if HOIST:
            parked = {}

            def h0():
                restore_hist()
                parked[2] = win_piece(2, 1)

            def h1():
                evac_front(2, parked[2], "act")
                parked[3] = win_piece(3, 1)

            def h2():
                evac_front(3, parked[3], "act")
                parked[4] = win_piece(4, 1)

            def h3():
                evac_front(4, parked[4], "act")
                conv(0)

            b_phase(0, bs0, [h0, h1, h2, h3])
        else:
            b_phase(0, bs0, [nohook] * 4)
        acc0 = w_phase(0, [(lambda: conv(1)), (lambda: conv(2)), (lambda: conv(3)), (lambda: None)] if HOIST else None)
        acc0.pump(0)
        acc0.finish_rest(next_gcol, next_dest, part="act")
        acc0_dve = lambda: acc0.finish_rest(next_gcol, next_dest, part="dve")
        if not HOIST:
            restore_hist()
            for i in (2, 3, 4):
                evac_front(i, win_piece(i, 1), "act")
        bs1 = g_rest(1, acc0_dve, conv_done=HOIST)
        b_phase(1, bs1, [nohook] * 4)
        acc1 = w_phase(1)
        acc1.pump(0)
        return lambda part="both": acc1.finish_rest(next_gcol, next_dest, part=part)

    xa_c0 = XA[0].ap.offset
    xa_stride = XA[1].ap.offset - XA[0].ap.offset
    assert XA[3].ap.offset == xa_c0 + 3 * xa_stride

    def ar_hist_a():
        return ar_t[:, xa_c0:xa_c0 + 4 * xa_stride].rearrange("p (c w) -> p c w", c=4)[:, :, 1:4]

    def ar_tail_a():
        return ar_t[:, xa_c0:xa_c0 + 4 * xa_stride].rearrange("p (c w) -> p c w", c=4)[:, :, S + 1:S + 4]

    xb_c0 = XB[0].ap.offset
    xb_stride = XB[1].ap.offset - XB[0].ap.offset

    def ar_hist_b():
        return ar_t[:, xb_c0:xb_c0 + 2 * xb_stride].rearrange("p (c w) -> p c w", c=2)[:, :, 0:16]

    def ar_tail_b():
        return ar_t[:, xb_c0:xb_c0 + 2 * xb_stride].rearrange("p (c w) -> p c w", c=2)[:, :, S:S + 16]

    id_d = nc.dram_tensor("ident", [128, 128], F32, kind="ExternalInput").ap()
    idf_t = sb("ident_f", [128, 128], F32)
    IDF = mk(idf_t[:], "identf")
    P.dma("sp", idf_t[:], id_d[:, :], misc[3], writes=[IDF])
    P.op("dve", lambda e: e.tensor_copy(out=ident_t[:], in_=idf_t[:]), reads=[IDF], writes=[IDENT])

    def gcol_of(st):
        l, kind = st
        return l * PL + {"f1": 0, "mx": 8, "f2": 16}[kind]

    def load_x_and_norm(t):
        for s in range(NS):
            for k in range(KC):
                P.dma("sp", X[k][s].ap, xT_d[k * 128:(k + 1) * 128, t * T + s * S:t * T + (s + 1) * S], x_ld[k][s],
                      writes=[X[k][s]])
        for s in range(NS):
            norm_standalone(s, gcol_of(stages[0]), H)

    last_is_ffn = stages[-1][1] != "mx"
    load_x_and_norm(0)
    for t in range(NT):
        dfr = None
        for idx, (l, kind) in enumerate(stages):
            is_last = idx + 1 == len(stages)
            if not is_last:
                ng, nd = gcol_of(stages[idx + 1]), H
            else:
                ng, nd = 2 * PL, YS
            if kind == "f1":
                dfr = ffn(l, 0, ng, nd, dfr, last=is_last)
            elif kind == "mx":
                dfr = mixer(l, t, ng, nd, dfr)
            else:
                dfr = ffn(l, 1, ng, nd, dfr, last=is_last)
        if last_is_ffn and t + 1 < NT:
            load_x_and_norm(t + 1)
            dfr()
        else:
            dfr()
            if t + 1 < NT:
                load_x_and_norm(t + 1)
        for s in range(NS):
            for k in range(KC):
                P.dma("sp", yT_d[k * 128:(k + 1) * 128, t * T + s * S:t * T + (s + 1) * S], YS[k][s].ap,
                      y_st[k][s], reads=[YS[k][s]])

    flush_casts(0)
    flush_stores(0)
    P.emit(final_waits=[(y_st[k][s], y_st[k][s].count) for k in range(KC) for s in range(NS)] + [(d, d.count) for d in ring_st if d.count > 0])
    es.close()
    stats = {e: len(P.ins[e]) for e in ENGS}
    stats["waits"] = P.n_wait
    return nc, stats


def _layout_weights(inp):
    f32 = np.float32

    def win_pieces(w):
        g = np.zeros((D, FC * 128), f32)
        u = np.zeros((D, FC * 128), f32)
        g[:, :DFF] = w[:, :DFF]
        u[:, :DFF] = w[:, DFF:]
        gu = np.stack([g, u], 0)
        gu = gu.reshape(2, KC, 128, FC, 128)
        return np.ascontiguousarray(gu.transpose(3, 2, 0, 1, 4)).reshape(FC, 128, PIECE)

    def wout_pieces(w):
        wp = np.zeros((FC * 128, D), f32)
        wp[:DFF] = w
        wp = wp.reshape(2, 11, 128, KC, 128)
        return np.ascontiguousarray(wp.transpose(3, 0, 2, 1, 4)).reshape(16, 128, WB_N)

    def pair_pieces(w, nout):
        wp = w.reshape(KC, 128, nout // 2, 2, 128)
        return np.ascontiguousarray(wp.transpose(2, 1, 3, 0, 4)).reshape(nout // 2, 128, PIECE)

    wA = np.stack([np.stack([win_pieces(inp["ffn1_w_in"][l]), win_pieces(inp["ffn2_w_in"][l])]) for l in range(L)])
    wB = np.stack([np.stack([wout_pieces(inp["ffn1_w_out"][l]), wout_pieces(inp["ffn2_w_out"][l])]) for l in range(L)])
    wC = np.stack([pair_pieces(inp["w_in"][l], 14) for l in range(L)])
    wD = np.stack([pair_pieces(inp["w_out"][l], 8) for l in range(L)])

    wS = np.zeros((L, 14, 128, 128), f32)
    for l in range(L):
        for c in range(4):
            for hh in range(2):
                h = 2 * c + hh
                sl = slice(hh * 64, (hh + 1) * 64)
                wS[l, c, sl, sl] = inp["rg_w_a"][l, h]
                wS[l, 4 + c, sl, sl] = inp["rg_w_x"][l, h]
        for c in range(2):
            for hh in range(2):
                g = 2 * c + hh
                sl = slice(hh * 64, (hh + 1) * 64)
                wS[l, 8 + c, sl, sl] = inp["pool_w"][l, g]
        for h in range(4):
            wS[l, 10 + h] = inp["sgu_w"][l, h].T
    wS = np.ascontiguousarray(wS.transpose(2, 0, 1, 3)).reshape(128, L * 14 * 128)

    pp = np.zeros((128, 2 * PL + 8), f32)

    def cols(v, n):
        return np.asarray(v, f32).reshape(n, 128).T

    for l in range(L):
        b = l * PL
        pp[:, b + 0:b + 8] = cols(inp["ffn1_norm"][l], 8)
        pp[:, b + 8:b + 16] = cols(inp["mix_norm"][l], 8)
        pp[:, b + 16:b + 24] = cols(inp["ffn2_norm"][l], 8)
        for tap in range(4):
            pp[:, b + 24 + tap * 4:b + 28 + tap * 4] = cols(inp["conv_w"][l, tap], 4)
        pp[:, b + 40:b + 44] = cols(inp["conv_b"][l], 4)
        pp[:, b + 44:b + 48] = cols(inp["rg_b_a"][l].reshape(-1), 4)
        pp[:, b + 48:b + 52] = cols(inp["rg_b_x"][l].reshape(-1), 4)
        pp[:, b + 52:b + 56] = cols(inp["lru_lambda"][l], 4)
        pp[:, b + 56:b + 58] = cols(inp["pool_scale"][l], 2)
        pp[:, b + 58:b + 60] = cols(inp["sgu_norm"][l], 2)
    pp[:, 2 * PL:2 * PL + 8] = cols(inp["final_norm"], 8)

    biasT = np.zeros((L, 2, 128, 128), f32)
    for l in range(L):
        for c in range(2):
            for hh in range(2):
                biasT[l, c, hh * 64:(hh + 1) * 64, :] = inp["sgu_b"][l, 2 * c + hh][None, :]
    biasT = np.ascontiguousarray(biasT.transpose(2, 0, 1, 3)).reshape(128, L * 2 * 128)

    cst = np.zeros((128, 128 + 2 + 32), f32)
    s_i = np.arange(128)[:, None]
    t_i = np.arange(128)[None, :]
    cst[:, 0:128] = (s_i <= t_i).astype(f32)
    wins = [(2, 4), (8, 16)]
    for c in range(2):
        wv = np.where(np.arange(128) < 64, wins[c][0], wins[c][1]).astype(f32)
        cst[:, 128 + c] = 1.0 / wv
        cnt = np.minimum(np.arange(16)[None, :] + 1, wv[:, None])
        cst[:, 130 + c * 16:130 + (c + 1) * 16] = 1.0 / cnt
    ident = np.eye(128, dtype=f32)
    return dict(wA=wA, wB=wB, wC=wC, wD=wD, wS=wS, pp=pp, biasT=biasT, cst=cst, ident=ident)


_CACHE = {}


def kernel(**inputs):
    inp = {k: np.asarray(v) for k, v in inputs.items()}
    x = inp["x"]
    B = x.shape[0]
    shared = _layout_weights(inp)
    if "nc" not in _CACHE:
        _CACHE["nc"] = build()[0]
    nc = _CACHE["nc"]
    in_maps = []
    for b in range(B):
        m = dict(shared)
        m["xT"] = np.ascontiguousarray(x[b].T)
        in_maps.append(m)
    res = run_bass_kernel_spmd(nc, in_maps, core_ids=list(range(B)))
    out = np.stack([np.ascontiguousarray(res.results[b]["yT"].T) for b in range(B)], 0)
    return out.astype(np.float32)
```
